# Optimizing a Trainium2 kernel written in Bass

```python
import math
import jax, jax.numpy as jnp
from jax import lax
import numpy as np

D_MODEL = 1024
BATCH = 8
SEQ = 4096
DEPTH = 1

HEAD_DIM = 64
MLA_HEADS = 8
MLA_Q_RANK = 256
MLA_KV_RANK = 128
MLA_NOPE_DIM = 64
MLA_ROPE_DIM = 32
MLA_V_DIM = HEAD_DIM
ROPE_THETA = 10000.0
DIL_HEADS = 8
DIL_PAIRS = ((128, 1), (512, 4), (2048, 16))
DIL_BLOCK = 128
DIL_WIDTH = DIL_HEADS * HEAD_DIM
MLA_WIDTH = MLA_HEADS * MLA_V_DIM
MIX_WIDTH = MLA_WIDTH + DIL_WIDTH
IN_SPLITS = (MLA_Q_RANK, MLA_KV_RANK, MLA_ROPE_DIM, DIL_WIDTH, DIL_WIDTH, DIL_WIDTH)
IN_WIDTH = sum(IN_SPLITS)
D_FF = 2816
CONV_WIDTH = 3
Q_BLOCK = 128
DN_ALPHA = (2.0 * DEPTH) ** 0.25
DN_BETA = (8.0 * DEPTH) ** -0.25
LN_EPS = 1e-5
RMS_EPS = 1e-6

kernel_name = "hybrid_mla_dilated_swa_convffn_deepnorm"


def layer_norm(x, g, b):
    xf = x.astype(jnp.float32)
    mu = jnp.mean(xf, axis=-1, keepdims=True)
    var = jnp.mean(jnp.square(xf - mu), axis=-1, keepdims=True)
    y = (xf - mu) * lax.rsqrt(var + LN_EPS) * g.astype(jnp.float32) + b.astype(jnp.float32)
    return y.astype(x.dtype)


def rms_norm(x, g):
    xf = x.astype(jnp.float32)
    y = xf * lax.rsqrt(jnp.mean(jnp.square(xf), axis=-1, keepdims=True) + RMS_EPS)
    return (y * g.astype(jnp.float32)).astype(x.dtype)


def apply_rope(x, pos):
    half = x.shape[-1] // 2
    freqs = ROPE_THETA ** (-jnp.arange(half, dtype=jnp.float32) / half)
    ang = pos.astype(jnp.float32)[:, None] * freqs[None, :]
    cos = jnp.cos(ang)[None, :, None, :]
    sin = jnp.sin(ang)[None, :, None, :]
    xf = x.astype(jnp.float32)
    x1, x2 = xf[..., :half], xf[..., half:]
    out = jnp.concatenate([x1 * cos - x2 * sin, x1 * sin + x2 * cos], axis=-1)
    return out.astype(x.dtype)


def alibi_slopes(n):
    return 2.0 ** (-8.0 * jnp.arange(1, n + 1, dtype=jnp.float32) / n)


def mla_attention(c_q, c_kv, k_rope, g_cq, g_ckv, w_uq, w_uk, w_uv):
    B, S, _ = c_q.shape
    pos = jnp.arange(S)
    c_q = rms_norm(c_q, g_cq)
    c_kv = rms_norm(c_kv, g_ckv)
    q = jnp.einsum('bsr,rhe->bshe', c_q, w_uq)
    q_nope, q_rope = q[..., :MLA_NOPE_DIM], q[..., MLA_NOPE_DIM:]
    k_nope = jnp.einsum('bsr,rhe->bshe', c_kv, w_uk)
    v = jnp.einsum('bsr,rhe->bshe', c_kv, w_uv)
    q_rope = apply_rope(q_rope, pos)
    k_rope = apply_rope(k_rope[:, :, None, :], pos)
    qf = jnp.concatenate([q_nope, q_rope], axis=-1)
    kf = jnp.concatenate([k_nope, jnp.broadcast_to(k_rope, k_nope.shape[:3] + (MLA_ROPE_DIM,))], axis=-1)
    scale = 1.0 / math.sqrt(MLA_NOPE_DIM + MLA_ROPE_DIM)
    nb = S // Q_BLOCK
    q_blocks = qf.reshape(B, nb, Q_BLOCK, MLA_HEADS, -1).transpose(1, 0, 2, 3, 4)
    k_pos = jnp.arange(S)

    def one_block(args):
        qb, bi = args
        s = jnp.einsum('bqhe,bkhe->bhqk', qb, kf).astype(jnp.float32) * scale
        q_pos = bi * Q_BLOCK + jnp.arange(Q_BLOCK)
        causal = q_pos[:, None] >= k_pos[None, :]
        s = jnp.where(causal[None, None], s, -jnp.inf)
        p = jax.nn.softmax(s, axis=-1).astype(v.dtype)
        return jnp.einsum('bhqk,bkhe->bqhe', p, v)

    o = lax.map(one_block, (q_blocks, jnp.arange(nb)))
    return o.transpose(1, 0, 2, 3, 4).reshape(B, S, MLA_HEADS * MLA_V_DIM)


def dilated_branch(q, k, v, slopes, window, dil):
    B, S, H, E = q.shape
    n_back = window // dil
    blk = DIL_BLOCK
    L = -(-S // (dil * blk)) * dil * blk
    M = L // dil
    nb = M // blk

    def to_sub(a):
        a = jnp.pad(a, ((0, 0), (0, L - S), (0, 0), (0, 0)))
        a = a.reshape(B, M, dil, H, E).transpose(0, 2, 1, 3, 4)
        return a.reshape(B, dil, nb, blk, H, E)

    def with_prev(ab):
        prev = jnp.pad(ab, ((0, 0), (0, 0), (1, 0), (0, 0), (0, 0), (0, 0)))[:, :, :-1]
        return jnp.concatenate([prev, ab], axis=3)

    qb = to_sub(q)
    kb = with_prev(to_sub(k))
    vb = with_prev(to_sub(v))
    s = jnp.einsum('bdnqhe,bdnkhe->bdnhqk', qb, kb).astype(jnp.float32) / math.sqrt(E)
    jq = jnp.arange(nb)[:, None] * blk + jnp.arange(blk)[None, :]
    jk = jnp.arange(nb)[:, None] * blk - blk + jnp.arange(2 * blk)[None, :]
    off = jq[:, :, None] - jk[:, None, :]
    valid = (off >= 0) & (off <= n_back) & (jk[:, None, :] >= 0)
    dist = (off * dil).astype(jnp.float32)
    bias = -slopes[None, :, None, None] * dist[:, None]
    s = jnp.where(valid[:, None], s + bias, -jnp.inf)
    lse = jax.nn.logsumexp(s, axis=-1)
    p = jnp.exp(s - lse[..., None]).astype(v.dtype)
    o = jnp.einsum('bdnhqk,bdnkhe->bdnqhe', p, vb)
    o = o.reshape(B, dil, M, H, E).transpose(0, 2, 1, 3, 4).reshape(B, L, H, E)[:, :S]
    lse = lse.transpose(0, 1, 2, 4, 3).reshape(B, dil, M, H).transpose(0, 2, 1, 3).reshape(B, L, H)[:, :S]
    return o, lse


def dilated_attention(q, k, v):
    B, S, H, E = q.shape
    slopes = alibi_slopes(H)
    outs, lses = [], []
    for window, dil in DIL_PAIRS:
        o, lse = dilated_branch(q, k, v, slopes, window, dil)
        outs.append(o.astype(jnp.float32))
        lses.append(lse)
    w = jax.nn.softmax(jnp.stack(lses, axis=0), axis=0)
    o = jnp.sum(w[..., None] * jnp.stack(outs, axis=0), axis=0)
    return o.astype(q.dtype).reshape(B, S, H * E)


def conv_gated_ffn(x, w_up, conv_w, conv_b, w_down):
    S = x.shape[1]
    u = x @ w_up
    y = conv_b
    for j in range(CONV_WIDTH):
        shift = CONV_WIDTH - 1 - j
        us = jnp.pad(u, ((0, 0), (shift, 0), (0, 0)))[:, :S] if shift else u
        y = y + conv_w[j] * us
    a, g = y[..., :D_FF], y[..., D_FF:]
    return (jax.nn.gelu(g) * a) @ w_down


def setup_inputs(seed: int = 0) -> dict:
    key = jax.random.key(seed)
    ks = jax.random.split(key, 17)
    f32 = jnp.float32
    x = jax.random.normal(ks[0], (BATCH, SEQ, D_MODEL), f32)
    col_scale = jnp.concatenate([jnp.ones((IN_WIDTH - DIL_WIDTH,), f32),
                                 jnp.full((DIL_WIDTH,), DN_BETA, f32)])
    w_in = jax.random.normal(ks[1], (D_MODEL, IN_WIDTH), f32) * D_MODEL ** -0.5 * col_scale
    g_cq = 1.0 + 0.02 * jax.random.normal(ks[2], (MLA_Q_RANK,), f32)
    g_ckv = 1.0 + 0.02 * jax.random.normal(ks[3], (MLA_KV_RANK,), f32)
    w_uq = jax.random.normal(ks[4], (MLA_Q_RANK, MLA_HEADS, MLA_NOPE_DIM + MLA_ROPE_DIM), f32) * MLA_Q_RANK ** -0.5
    w_uk = jax.random.normal(ks[5], (MLA_KV_RANK, MLA_HEADS, MLA_NOPE_DIM), f32) * MLA_KV_RANK ** -0.5
    w_uv = jax.random.normal(ks[6], (MLA_KV_RANK, MLA_HEADS, MLA_V_DIM), f32) * MLA_KV_RANK ** -0.5 * DN_BETA
    w_o = jax.random.normal(ks[7], (MIX_WIDTH, D_MODEL), f32) * MIX_WIDTH ** -0.5 * DN_BETA
    ln1_g = 1.0 + 0.02 * jax.random.normal(ks[8], (D_MODEL,), f32)
    ln1_b = 0.02 * jax.random.normal(ks[9], (D_MODEL,), f32)
    w_up = jax.random.normal(ks[10], (D_MODEL, 2 * D_FF), f32) * D_MODEL ** -0.5 * DN_BETA
    conv_w = jax.random.normal(ks[11], (CONV_WIDTH, 2 * D_FF), f32) * CONV_WIDTH ** -0.5
    conv_b = 0.01 * jax.random.normal(ks[12], (2 * D_FF,), f32)
    w_down = jax.random.normal(ks[13], (D_FF, D_MODEL), f32) * D_FF ** -0.5 * DN_BETA
    ln2_g = 1.0 + 0.02 * jax.random.normal(ks[14], (D_MODEL,), f32)
    ln2_b = 0.02 * jax.random.normal(ks[15], (D_MODEL,), f32)
    return {"x": x, "w_in": w_in, "g_cq": g_cq, "g_ckv": g_ckv, "w_uq": w_uq,
            "w_uk": w_uk, "w_uv": w_uv, "w_o": w_o, "ln1_g": ln1_g, "ln1_b": ln1_b,
            "w_up": w_up, "conv_w": conv_w, "conv_b": conv_b, "w_down": w_down,
            "ln2_g": ln2_g, "ln2_b": ln2_b}


def reference(x, w_in, g_cq, g_ckv, w_uq, w_uk, w_uv, w_o, ln1_g, ln1_b,
              w_up, conv_w, conv_b, w_down, ln2_g, ln2_b):
    B, S, _ = x.shape
    for _layer in range(DEPTH):
        h = x @ w_in
        idx = np.cumsum(IN_SPLITS)[:-1].tolist()
        c_q, c_kv, k_rope, q_d, k_d, v_d = jnp.split(h, idx, axis=-1)
        o_mla = mla_attention(c_q, c_kv, k_rope, g_cq, g_ckv, w_uq, w_uk, w_uv)
        o_dil = dilated_attention(q_d.reshape(B, S, DIL_HEADS, HEAD_DIM),
                                  k_d.reshape(B, S, DIL_HEADS, HEAD_DIM),
                                  v_d.reshape(B, S, DIL_HEADS, HEAD_DIM))
        mix = jnp.concatenate([o_mla, o_dil], axis=-1) @ w_o
        x = layer_norm(DN_ALPHA * x + mix, ln1_g, ln1_b)
        ffn = conv_gated_ffn(x, w_up, conv_w, conv_b, w_down)
        x = layer_norm(DN_ALPHA * x + ffn, ln2_g, ln2_b)
    return x
```

```python
import math
from contextlib import ExitStack

import numpy as np
import concourse.bass as bass
import concourse.mybir as mybir
from concourse.bass_utils import run_bass_kernel_spmd

F32 = mybir.dt.float32
BF16 = mybir.dt.bfloat16
ALU = mybir.AluOpType
AF = mybir.ActivationFunctionType

S = 4096
D = 1024
DFF = 2816
NJ = 22
DN_ALPHA = 2.0 ** 0.25
LN_EPS = 1e-5
RMS_EPS = 1e-6
DILS = (1, 4, 16)
KB = 1024

SAME_ENG_SYNC = True
ENGS = ("pe", "act", "dve", "pool", "sp")


class _Op:
    __slots__ = ("eng", "fn", "deps", "dma", "signal", "sem", "val", "prev", "strict")


class Sched:
    def __init__(self, nc, stack, n_dma=36):
        self.nc = nc
        self.ops = []
        self.flushed = 0
        self.last_w = {}
        self.rd_eng = {}
        self.rd_dma = {}
        self.sems = {}
        for e in ENGS:
            self.sems[("e", e)] = stack.enter_context(nc.semaphore("s_" + e))
        for i in range(n_dma):
            self.sems[("d", i)] = stack.enter_context(nc.semaphore("d%d" % i))
        self.n_dma = n_dma
        self.eng_cnt = {e: 0 for e in ENGS}
        self.dma_val = [0] * n_dma
        self.dma_rr = 0
        self.sw_rr = 0
        self.n_hw = 20
        self.known = {e: {} for e in ENGS}
        self.log = {}

    def add(self, eng, fn, reads=(), writes=(), dma=False, strict=False):
        i = len(self.ops)
        if eng != "pe":
            writes = list(writes) + [r for r in reads if isinstance(r, tuple) and r[0] == "ps" and r not in writes]
        deps = set()
        for r in reads:
            if r in self.last_w:
                deps.add(self.last_w[r])
        for w in writes:
            if w in self.last_w:
                deps.add(self.last_w[w])
            deps.update(self.rd_eng.get(w, {}).values())
            deps.update(self.rd_dma.get(w, ()))
        for r in reads:
            if dma:
                self.rd_dma.setdefault(r, []).append(i)
            else:
                self.rd_eng.setdefault(r, {})[eng] = i
        for w in writes:
            self.last_w[w] = i
            self.rd_eng[w] = {}
            self.rd_dma[w] = []
        op = _Op()
        op.eng, op.fn, op.deps, op.dma = eng, fn, deps, dma
        op.signal, op.sem, op.val, op.prev = False, None, 0, 0
        op.strict = strict
        self.ops.append(op)
        return i

    def _skip(self, p, op):
        if p.dma or op.dma:
            return False
        if p.eng != op.eng:
            return False
        if p.eng == "pe":
            return True
        return not (SAME_ENG_SYNC or p.strict or op.strict)

    def flush(self):
        nc = self.nc
        ops = self.ops[self.flushed:]
        self.flushed = len(self.ops)
        for op in ops:
            for d in op.deps:
                p = self.ops[d]
                if not p.dma and not self._skip(p, op):
                    p.signal = True
        last = {}
        for op in ops:
            if not op.dma:
                last[op.eng] = op
        for op in last.values():
            op.signal = True
        for op in ops:
            if op.dma:
                if op.eng == "pool":
                    k = self.n_hw + self.sw_rr
                    self.sw_rr = (self.sw_rr + 1) % (self.n_dma - self.n_hw)
                else:
                    k = self.dma_rr
                    self.dma_rr = (k + 1) % self.n_hw
                op.sem = ("d", k)
                op.prev = self.dma_val[k]
                self.dma_val[k] += 16
                op.val = self.dma_val[k]
            elif op.signal:
                self.eng_cnt[op.eng] += 1
                op.sem = ("e", op.eng)
                op.val = self.eng_cnt[op.eng]

        def emit(name, e):
            known = self.known[name]

            def wait(key, val):
                if val <= 0 or known.get(key, 0) >= val:
                    return
                e.wait_ge(self.sems[key], val)
                known[key] = val
                self.log.setdefault(name, []).append(("w", key, val))

            for op in ops:
                if op.eng != name:
                    continue
                need = {}
                for d in op.deps:
                    p = self.ops[d]
                    if p.sem is None or self._skip(p, op):
                        continue
                    need[p.sem] = max(need.get(p.sem, 0), p.val)
                if op.dma and op.prev > 0:
                    need[op.sem] = max(need.get(op.sem, 0), op.prev)
                for k, v in need.items():
                    wait(k, v)
                ins = op.fn(e)
                if op.dma:
                    ins.then_inc(self.sems[op.sem], 16)
                    self.log.setdefault(name, []).append(("i", op.sem, 16))
                elif op.signal:
                    ins.then_inc(self.sems[op.sem], 1)
                    self.log.setdefault(name, []).append(("i", op.sem, 1))
            for en in ENGS:
                if en != name:
                    wait(("e", en), self.eng_cnt[en])
            for k in range(self.n_dma):
                wait(("d", k), self.dma_val[k])

        with nc.Block() as block:
            @block.tensor
            def _(e):
                emit("pe", e)

            @block.scalar
            def _(e):
                emit("act", e)

            @block.vector
            def _(e):
                emit("dve", e)

            @block.gpsimd
            def _(e):
                emit("pool", e)

            @block.sync
            def _(e):
                emit("sp", e)

        self.last_w.clear()
        self.rd_eng.clear()
        self.rd_dma.clear()


def _dsize(dtype):
    return 4 if dtype == F32 else 2


def build(stop_after=5, dumps=()):
    nc = bass.Bass("TRN2", target_bir_lowering=False)

    def din(name, shape, dtype=F32):
        return nc.dram_tensor(name, shape, dtype, kind="ExternalInput").ap()

    xT_d = din("xT", [D, S])
    x_d = din("x", [S, D])
    w1_d = din("w1", [D, 2176])
    wq_d = din("wq", [256, 1024])
    wuk_d = din("wuk", [128, 512])
    wuv_d = din("wuv", [128, 512])
    gq_d = din("gq", [128, 4])
    qtab_d = din("qtab", [128, S])
    ktab_d = din("ktab", [2, 128, S])
    tri_d = din("tri", [128, 128])
    mtab_d = din("mtab", [128, 24 * 256])
    wo_d = din("wo", [D, D])
    wup_d = din("wup", [NJ, D, 256])
    cw_d = din("cw", [128, 44 * 4])
    wd_d = din("wd", [DFF, D])
    lnp_d = din("lnp", [4, 128, D])
    ident_d = din("ident", [128, 128])
    out_d = nc.dram_tensor("out", [S, D], F32, kind="ExternalOutput").ap()
    qd_s = nc.dram_tensor("qd_s", [512, S], BF16, kind="Internal").ap()
    kd_s = nc.dram_tensor("kd_s", [512, S], BF16, kind="Internal").ap()
    qk4_s = [nc.dram_tensor(n_, [512, S], BF16, kind="Internal").ap() for n_ in ("qd4_s", "kd4_s")]
    qk16_s = [nc.dram_tensor(n_, [512, S], BF16, kind="Internal").ap() for n_ in ("qd16_s", "kd16_s")]
    vd_s = nc.dram_tensor("vd_s", [S, 512], BF16, kind="Internal").ap()
    x1_s = nc.dram_tensor("x1_s", [S, D], F32, kind="Internal").ap()

    dump_out = {}

    with ExitStack() as stack:
        ARENA_BYTES = 206 * KB
        arena = stack.enter_context(nc.sbuf_tensor("arena", [128, ARENA_BYTES // 2], BF16))
        psum = stack.enter_context(nc.psum_tensor("psum", [128, 4096], F32))
        banks = [psum[:, i * 512:(i + 1) * 512] for i in range(8)]
        sch = Sched(nc, stack)

        def carve(off, shape, dtype):
            n = 1
            for v in shape:
                n *= v
            nb = n * _dsize(dtype)
            assert off % 4 == 0 and off + nb <= ARENA_BYTES, (off, nb)
            ap = arena[:, off // 2:(off + nb) // 2]
            if dtype != BF16:
                ap = ap.bitcast(dtype)
            if len(shape) == 2:
                ap = ap.rearrange("p (a b) -> p a b", a=shape[0], b=shape[1])
            elif len(shape) == 3:
                ap = ap.rearrange("p (a b c) -> p a b c", a=shape[0], b=shape[1], c=shape[2])
            return ap

        class Bump:
            def __init__(self, start, end):
                self.off, self.end = start, end

            def __call__(self, shape, dtype):
                n = 1
                for v in shape:
                    n *= v
                nb = (n * _dsize(dtype) + 63) // 64 * 64
                ap = carve(self.off, shape, dtype)
                self.off += nb
                assert self.off <= self.end, (self.off, self.end)
                return ap

        A0, B0, C0, CEND = 0, 64 * KB, 128 * KB, ARENA_BYTES

        def mm(out, lhsT, rhs, start, stop, reads, writes):
            sch.add("pe", lambda e: e.matmul(out, lhsT=lhsT, rhs=rhs, start=start, stop=stop), reads, writes)

        def tr(out, in_, ident, reads, writes):
            sch.add("pe", lambda e: e.transpose(out, in_, ident), reads, writes)

        def act(out, in_, func, reads, writes, scale=None, bias=None, strict=False):
            kw = {}
            if scale is not None:
                kw["scale"] = scale
            if bias is not None:
                kw["bias"] = bias
            sch.add("act", lambda e: e.activation(out=out, in_=in_, func=func, **kw), reads, writes, strict=strict)

        def tt(eng, out, in0, in1, op, reads, writes, strict=False):
            sch.add(eng, lambda e: e.tensor_tensor(out=out, in0=in0, in1=in1, op=op), reads, writes, strict=strict)

        def stt(eng, out, in0, scalar, in1, op0, op1, reads, writes, strict=False):
            sch.add(eng, lambda e: e.scalar_tensor_tensor(out=out, in0=in0, scalar=scalar, in1=in1,
                                                          op0=op0, op1=op1), reads, writes, strict=strict)

        def ts(eng, out, in0, s1, s2, op0, op1, reads, writes, strict=False):
            sch.add(eng, lambda e: e.tensor_scalar(out=out, in0=in0, scalar1=s1, scalar2=s2, op0=op0, op1=op1),
                    reads, writes, strict=strict)

        def cp(eng, out, in_, reads, writes, strict=False):
            if eng == "act":
                act(out, in_, AF.Copy, reads, writes, strict=strict)
            else:
                sch.add(eng, lambda e: e.tensor_copy(out=out, in_=in_), reads, writes, strict=strict)

        def recip(out, in_, reads, writes):
            sch.add("dve", lambda e: e.reciprocal(out=out, in_=in_), reads, writes, strict=True)

        def memset(eng, ap, val, writes):
            sch.add(eng, lambda e: e.memset(ap, val), (), writes)

        def dma(eng, out, in_, reads, writes):
            sch.add(eng, lambda e: e.dma_start(out=out, in_=in_), reads, writes, dma=True)

        def dump(name, ap, shape, dtype, reads):
            if name not in dumps:
                return
            o = nc.dram_tensor("dbg_" + name, shape, dtype, kind="ExternalOutput").ap()
            dump_out[name] = o
            dma("sp", o, ap, reads, [("dbg", name)])

        evac_rr = [0]

        def evac(out, in_, reads, writes):
            evac_rr[0] ^= 1
            cp("act" if evac_rr[0] else "dve", out, in_, reads, writes)

        cst = Bump(CEND - 2 * KB, CEND)
        epsr = cst([1], F32)
        epsl = cst([1], F32)
        onesf = cst([128], F32)
        identb = cst([128], BF16)
        memset("pool", epsr, RMS_EPS, ["epsr"])
        memset("pool", epsl, LN_EPS, ["epsl"])
        memset("pool", onesf, 1.0, ["onesf"])
        dma("pool", identb, ident_d, (), ["ident"])
        CTOP = CEND - 2 * KB

        cq = carve(C0, [2, S], BF16)
        ckv = carve(C0 + 16 * KB, [S], BF16)
        krd = carve(C0 + 24 * KB, [S], BF16)

        a = Bump(A0, B0)
        c = Bump(C0 + 32 * KB, CTOP)
        w1b = a([8, 2176], BF16)
        xt = [a([8, 512], BF16) for _ in range(2)]
        stg = [a([512], BF16) for _ in range(4)]
        stg4 = [a([512], BF16) for _ in range(4)]
        P16 = carve(B0, [8, S], BF16)
        s4_rr = [0]
        gq = c([4], F32)
        raw = c([3, 512], F32)
        sq = c([3, 512], F32)
        rstd = c([2, 512], F32)
        lnt = c([2, 512], F32)
        ktab = [c([2, 512], F32) for _ in range(2)]
        t1 = c([512], F32)
        t2 = c([512], F32)

        w1v = w1_d.rearrange("(k p) n -> p k n", p=128)
        xTv = xT_d.rearrange("(k p) t -> p k t", p=128)
        for k in range(0, 8, 4):
            dma("pool", xt[0][:, k:k + 4, :], xTv[:, k:k + 4, 0:512], (), [("xt", 0, k // 4)])
        for k in range(8):
            dma("pool", w1b[:, k, 0:640], w1v[:, k, 0:640], (), [("w1b", k)])
        for k in range(8):
            dma("pool", w1b[:, k, 640:2176], w1v[:, k, 640:2176], (), [("w1c", k)])
        dma("sp", gq, gq_d, (), ["gq"])
        ktv = ktab_d.rearrange("a p t -> p a t")
        w1r = [("w1b", k) for k in range(8)]
        bank_rr = [0]
        stg_rr = [0]

        def next_bank(n=6):
            b = bank_rr[0]
            bank_rr[0] = (b + 1) % n
            return b

        def store_p16(hf_):
            for g in range(8):
                r0 = (g % 4) * 128
                dv = qk16_s[g // 4][r0:r0 + 128, :].rearrange("p (r j) -> p r j", r=16)
                sv = P16[:, g, :].rearrange("p (r j) -> p r j", r=16)
                for rh in range(2):
                    dma("sp", dv[:, rh * 8:(rh + 1) * 8, hf_ * 128:(hf_ + 1) * 128],
                        sv[:, rh * 8:(rh + 1) * 8, hf_ * 128:(hf_ + 1) * 128],
                        [("P16", g, tt_) for tt_ in range(hf_ * 4, hf_ * 4 + 4)], [("qkd16", g, hf_, rh)])

        for t in range(8):
            xb = t % 2
            tok = slice(t * 512, (t + 1) * 512)
            if t > 0:
                for k in range(0, 8, 4):
                    dma("pool", xt[xb][:, k:k + 4, :], xTv[:, k:k + 4, tok], (), [("xt", xb, k // 4)])
            dma("sp", ktab[xb][64:128], ktv[64:128, :, tok], (), [("ktab", xb)])
            xr = [("xt", xb, 0), ("xt", xb, 1)]
            for gi in list(range(5, 13)) + list(range(5)):
                if gi == 0 and t % 4 == 3:
                    store_p16(t // 4)
                b = next_bank()
                ps = banks[b]
                for k in range(8):
                    mm(ps[:, :], w1b[:, k, gi * 128:(gi + 1) * 128], xt[xb][:, k, :], k == 0, k == 7,
                       xr + [("w1b" if gi < 5 else "w1c", k)], [("ps", b)])
                if gi < 3:
                    act(raw[:, gi, :], ps[:, :], AF.Copy, [("ps", b)], [("raw", gi)])
                    act(sq[:, gi, :], ps[:, :], AF.Square, [("ps", b)], [("sq", gi)])
                    if gi == 1:
                        mm(banks[6][:, :], onesf, sq[:, 0, :], True, False, ["onesf", ("sq", 0)], [("ps", 6)])
                        mm(banks[6][:, :], onesf, sq[:, 1, :], False, True, ["onesf", ("sq", 1)], [("ps", 6)])
                        act(lnt[:, 0, :], banks[6][:, :], AF.Ln, [("ps", 6), "epsr"], [("lnt", 0)],
                            scale=1.0 / 256, bias=epsr)
                        act(rstd[:, 0, :], lnt[:, 0, :], AF.Exp, [("lnt", 0)], [("rstd", 0)], scale=-0.5)
                        for cc in range(2):
                            stt("dve", cq[:, cc, tok], raw[:, cc, :], gq[:, cc:cc + 1], rstd[:, 0, :],
                                ALU.mult, ALU.mult, [("raw", cc), "gq", ("rstd", 0)], [("cq", t)])
                    if gi == 2:
                        mm(banks[7][:, :], onesf, sq[:, 2, :], True, True, ["onesf", ("sq", 2)], [("ps", 7)])
                        act(lnt[:, 1, :], banks[7][:, :], AF.Ln, [("ps", 7), "epsr"], [("lnt", 1)],
                            scale=1.0 / 128, bias=epsr)
                        act(rstd[:, 1, :], lnt[:, 1, :], AF.Exp, [("lnt", 1)], [("rstd", 1)], scale=-0.5)
                        stt("dve", ckv[:, tok], raw[:, 2, :], gq[:, 2:3], rstd[:, 1, :],
                            ALU.mult, ALU.mult, [("raw", 2), "gq", ("rstd", 1)], [("ckv", t)])
                elif gi == 3:
                    tt("dve", t1[64:128, :], ps[64:128, :], ktab[xb][64:128, 0, :], ALU.mult,
                       [("ps", b), ("ktab", xb)], ["t1"])
                elif gi == 4:
                    tt("dve", t2[64:128, :], ps[64:128, :], ktab[xb][64:128, 1, :], ALU.mult,
                       [("ps", b), ("ktab", xb)], ["t2"])
                    tt("dve", krd[64:128, tok], t1[64:128, :], t2[64:128, :], ALU.add, ["t1", "t2"], [("krd", t)])
                else:
                    si = stg_rr[0]
                    stg_rr[0] = (si + 1) % 4
                    g = gi - 5
                    eng_ = "act" if g % 2 == 0 else "dve"
                    cp(eng_, stg[si], ps[:, :], [("ps", b)], [("stg", si)])
                    dst = qd_s if gi < 9 else kd_s
                    r0 = (g % 4) * 128
                    dma("sp", dst[r0:r0 + 128, tok], stg[si], [("stg", si)], [("qkd", gi, t)])
                    s4 = s4_rr[0]
                    s4_rr[0] = (s4 + 1) % 4
                    cp(eng_, stg4[s4].rearrange("p (r j) -> p r j", r=4), ps.rearrange("p (j r) -> p r j", r=4),
                       [("ps", b)], [("stg4", s4)])
                    d4 = qk4_s[0 if gi < 9 else 1][r0:r0 + 128, :].rearrange("p (r j) -> p r j", r=4)
                    dma("sp", d4[:, :, t * 128:(t + 1) * 128], stg4[s4].rearrange("p (r j) -> p r j", r=4),
                        [("stg4", s4)], [("qkd4", gi, t)])
                    cp(eng_, P16[:, g, :].rearrange("p (r j) -> p r j", r=16)[:, :, t * 32:(t + 1) * 32],
                       ps.rearrange("p (j r) -> p r j", r=16), [("ps", b)], [("P16", g, t)])
            for sub in range(4):
                b = next_bank()
                ps = banks[b]
                for k in range(8):
                    mm(ps[:, :], xt[xb][:, k, sub * 128:(sub + 1) * 128], w1b[:, k, 1664:2176], k == 0, k == 7,
                       xr + [("w1c", k)], [("ps", b)])
                si = stg_rr[0]
                stg_rr[0] = (si + 1) % 4
                evac(stg[si], ps[:, :], [("ps", b)], [("stg", si)])
                r0 = t * 512 + sub * 128
                dma("sp", vd_s[r0:r0 + 128, :], stg[si], [("stg", si)], [("vd", t, sub)])
        dump("cq", cq, [128, 2, S], BF16, [("cq", t) for t in range(8)])
        dump("ckv", ckv, [128, S], BF16, [("ckv", t) for t in range(8)])
        dump("krd", krd[64:128], [64, S], BF16, [("krd", t) for t in range(8)])
        sch.flush()
        if "qd_s" in dumps:
            for nm, src in (("qd_s", qd_s), ("kd_s", kd_s)):
                o = nc.dram_tensor("dbg_" + nm, [512, S], BF16, kind="ExternalOutput").ap()
                dump_out[nm] = o
                dma("sp", o, src, (), [("dbg", nm)])
            o = nc.dram_tensor("dbg_vd_s", [S, 512], BF16, kind="ExternalOutput").ap()
            dump_out["vd_s"] = o
            dma("sp", o, vd_s, (), [("dbg", "vd_s")])
            sch.flush()

        attnT = carve(B0, [8, S], BF16)

        if stop_after >= 2:
            a = Bump(A0, B0)
            c = Bump(C0 + 32 * KB, CTOP)
            QT = [a([S], BF16) for _ in range(2)]
            KT = [a([S], BF16) for _ in range(2)]
            VH = [a([32, 128], BF16) for _ in range(2)]
            qtab = c([S], F32)
            wqb = c([2, 1024], BF16)
            wukb = c([512], BF16)
            wuvb = c([512], BF16)
            trib = c([128], BF16)
            rec = [c([512], F32) for _ in range(2)]
            PT = [c([512], BF16) for _ in range(4)]
            dma("sp", qtab, qtab_d, (), ["qtab"])
            dma("pool", wqb, wq_d.rearrange("(k p) n -> p k n", p=128), (), ["wqb"])
            dma("pool", wukb, wuk_d, (), ["wukb"])
            dma("pool", wuvb, wuv_d, (), ["wuvb"])
            dma("pool", trib, tri_d, (), ["trib"])
            for hb in range(2):
                memset("pool", VH[hb][:, :, 64:128], 1.0, [("VH1", hb)])
            sc_mla = 1.0 / math.sqrt(96.0)
            prep_rr = [0]

            def prep_units(h):
                hb = h % 2
                units = []

                def q_unit(t):
                    tok = slice(t * 512, (t + 1) * 512)
                    b = 5 + prep_rr[0]
                    prep_rr[0] ^= 1
                    for k in range(2):
                        mm(banks[b][:, :], wqb[:, k, h * 128:(h + 1) * 128], cq[:, k, tok], k == 0, k == 1,
                           ["wqb", "cq"], [("ps", b)])
                    tt("dve", QT[hb][:, tok], banks[b][:, :], qtab[:, tok], ALU.mult,
                       [("ps", b), "qtab"], [("QT", hb)])

                def k_unit(t):
                    tok = slice(t * 512, (t + 1) * 512)
                    b = 5 + prep_rr[0]
                    prep_rr[0] ^= 1
                    mm(banks[b][0:64, :], wukb[:, h * 64:(h + 1) * 64], ckv[:, tok], True, True,
                       ["wukb", "ckv"], [("ps", b)])
                    cp("dve", KT[hb][0:64, tok], banks[b][0:64, :], [("ps", b)], [("KT", hb)])

                def kr_unit():
                    cp("pool", KT[hb][64:128, :], krd[64:128, :], ["krd"], [("KT", hb)])

                def v_unit(grp):
                    for i in range(8):
                        kt = grp * 8 + i
                        mm(banks[7][:, i * 64:(i + 1) * 64], ckv[:, kt * 128:(kt + 1) * 128],
                           wuvb[:, h * 64:(h + 1) * 64], True, True, ["wuvb", "ckv"], [("ps", 7)])
                    cp("dve", VH[hb][:, grp * 8:(grp + 1) * 8, 0:64],
                       banks[7][:, :].rearrange("p (a b) -> p a b", a=8, b=64), [("ps", 7)], [("VH", hb)])

                units.append(kr_unit)
                for t in range(8):
                    units.append(lambda t=t: q_unit(t))
                    units.append(lambda t=t: k_unit(t))
                for grp in range(4):
                    units.append(lambda grp=grp: v_unit(grp))
                return units

            st_rr = [0]
            pt_rr = [0]

            def attn(h, units):
                hb = h % 2
                tiles = []
                for qc in range(8):
                    for kt in range(4 * qc + 4):
                        tiles.append((qc, kt))
                pend = []

                def do_pv(item):
                    qc, kt, n0, N, pb = item
                    ob = 3 + (qc % 2)
                    mm(banks[ob][:, n0:512], VH[hb][:, kt, :], PT[pb][:, 0:N], kt == 0, kt == 4 * qc + 3,
                       [("VH", hb), ("VH1", hb), ("PT", pb)], [("ps", ob)])
                    if kt == 4 * qc + 3:
                        rb = qc % 2
                        recip(rec[rb][0:64, :], banks[ob][64:128, :], [("ps", ob)], [("rec", rb)])
                        po = (h % 2) * 64
                        tt("dve", attnT[po:po + 64, h // 2, qc * 512:(qc + 1) * 512], banks[ob][0:64, :],
                           rec[rb][0:64, :], ALU.mult, [("ps", ob), ("rec", rb)], [("attnT", h, qc)])

                for (qc, kt) in tiles:
                    n0 = max(0, kt * 128 - qc * 512)
                    N = 512 - n0
                    sb = st_rr[0]
                    st_rr[0] = (sb + 1) % 3
                    pb = pt_rr[0]
                    pt_rr[0] = (pb + 1) % 4
                    diag = kt >= 4 * qc
                    mm(banks[sb][:, 0:N], KT[hb][:, kt * 128:(kt + 1) * 128],
                       QT[hb][:, qc * 512 + n0:(qc + 1) * 512], True, not diag,
                       [("KT", hb), ("QT", hb)], [("ps", sb)])
                    if diag:
                        mm(banks[sb][:, 0:128], identb, trib, False, True, ["ident", "trib"], [("ps", sb)])
                    act(PT[pb][:, 0:N], banks[sb][:, 0:N], AF.Exp, [("ps", sb)], [("PT", pb)], scale=sc_mla)
                    pend.append((qc, kt, n0, N, pb))
                    if len(pend) > 2:
                        do_pv(pend.pop(0))
                    n_t[0] += 1
                    if units and n_t[0] % 5 == 0:
                        units.pop(0)()
                while pend:
                    do_pv(pend.pop(0))
                while units:
                    units.pop(0)()

            n_t = [0]
            for u in prep_units(0):
                u()
            for h in range(8):
                attn(h, prep_units(h + 1) if h + 1 < 8 else [])
            dump("QT0", QT[0], [128, S], BF16, [("QT", 0)])
            dump("KT0", KT[0], [128, S], BF16, [("KT", 0)])
            dump("VH0", VH[0], [128, 32, 128], BF16, [("VH", 0)])
            dump("attn_mla", attnT[:, 0:4, :], [128, 4, S], BF16,
                 [("attnT", h, qc) for h in range(8) for qc in range(8)])
            sch.flush()

        if stop_after >= 3:
            a = Bump(A0, B0)
            c = Bump(C0, CTOP)
            qh = [a([S], BF16) for _ in range(2)]
            kh = [a([S], BF16) for _ in range(2)]
            VD = [a([32, 128], BF16) for _ in range(3)]
            rect = a([1024], F32)
            acc = [c([S], F32) for _ in range(2)]
            btab = [c([3, 256], BF16) for _ in range(2)]
            PTd = [c([512], BF16) for _ in range(4)]
            qkp = {1: (c([S], BF16), c([S], BF16)), 2: (c([S], BF16), c([S], BF16))}
            mtv = mtab_d.rearrange("p (a b) -> p a b", a=24, b=256)
            for di_ in (1, 2):
                for w_ in range(2):
                    memset("pool", qkp[di_][w_][64:128, :], 0.0, [("qkpz", di_, w_)])

            def load_btab(h):
                dma("pool", btab[h % 2], mtv[:, 3 * h:3 * h + 3, :], (), [("btab", h % 2)])

            def load_perm(h, di):
                src = qk4_s if di == 1 else qk16_s
                for w_ in range(2):
                    dma("sp", qkp[di][w_][0:64, :], src[w_][h * 64:(h + 1) * 64, :], (), [("qkp", di, w_)])
            for di in range(3):
                memset("pool", VD[di][:, :, 64:128], 1.0, [("VD1", di)])
            for b_ in range(2):
                memset("pool", qh[b_][64:128, :], 0.0, [("qz", b_)])
                memset("pool", kh[b_][64:128, :], 0.0, [("kz", b_)])

            def load_qk(h):
                hb = h % 2
                dma("sp", qh[hb][0:64, :], qd_s[h * 64:(h + 1) * 64, :], (), [("qh", hb)])
                dma("sp", kh[hb][0:64, :], kd_s[h * 64:(h + 1) * 64, :], (), [("kh", hb)])

            def load_v(h, di):
                dil = DILS[di]
                nb = 32 // dil
                src = vd_s.rearrange("(kb p r) c -> p r kb c", kb=nb, p=128, r=dil)
                dst = VD[di].rearrange("p (r kb) e -> p r kb e", r=dil, kb=nb)
                cols = slice(h * 64, (h + 1) * 64)
                if dil == 1:
                    for g in range(4):
                        dma("sp", dst[:, 0, g * 8:(g + 1) * 8, 0:64], src[:, 0, g * 8:(g + 1) * 8, cols], (), [("VD", di)])
                elif dil == 4:
                    for r in range(4):
                        dma("sp", dst[:, r, :, 0:64], src[:, r, :, cols], (), [("VD", di)])
                else:
                    for kb in range(2):
                        for g in range(2):
                            dma("sp", dst[:, g * 8:(g + 1) * 8, kb, 0:64], src[:, g * 8:(g + 1) * 8, kb, cols],
                                (), [("VD", di)])

            st_rr3 = [0]
            pt_rr3 = [0]
            ob_rr = [0]

            def dil_tiles(h, di):
                hb = h % 2
                dil = DILS[di]
                nb = 32 // dil
                gbank = {}
                gdone = {}

                def slot(r, n):
                    if dil == 16:
                        return r // 2, (r % 2) * 256 + n * 128
                    return r * (nb // 4) + n // 4, (n % 4) * 128

                def obank(gid):
                    if gid not in gbank:
                        gbank[gid] = 3 + ob_rr[0]
                        ob_rr[0] = (ob_rr[0] + 1) % 5
                        gdone[gid] = 0
                    return gbank[gid]

                def finish(gid):
                    ob = gbank[gid]
                    if dil == 1:
                        av = acc[hb][:, gid * 512:(gid + 1) * 512]
                        ov = banks[ob]
                    elif dil == 4:
                        r, g = gid // 2, gid % 2
                        av = acc[hb].rearrange("p (i r) -> p r i", r=4)[:, r, g * 512:(g + 1) * 512]
                        ov = banks[ob]
                    else:
                        av = acc[hb].rearrange("p (i r) -> p r i", r=16)[:, 2 * gid:2 * gid + 2, :]
                        ov = banks[ob].rearrange("p (a b) -> p a b", a=2, b=256)
                    tt("dve", av, ov, av, ALU.add, [("ps", ob), ("acc", hb)], [("acc", hb)], strict=True)

                def do_pv(item):
                    r, kb, pb, c0 = item
                    blk = r * nb + kb
                    rd = [("VD", di), ("VD1", di), ("PTd", pb)]
                    gid, col = slot(r, kb)
                    ob = obank(gid)
                    mm(banks[ob][:, col:col + 128], VD[di][:, blk, :], PTd[pb][:, c0:c0 + 128], kb == 0, True,
                       rd, [("ps", ob)])
                    if kb + 1 < nb:
                        gid2, col2 = slot(r, kb + 1)
                        ob2 = obank(gid2)
                        mm(banks[ob2][:, col2:col2 + 128], VD[di][:, blk, :], PTd[pb][:, c0 + 128:c0 + 256], True, False,
                           rd, [("ps", ob2)])
                    gdone[gid] += 1
                    if gdone[gid] == 4:
                        finish(gid)

                tiles = [(r, kb) for r in range(dil) for kb in range(nb)]
                pend = []
                for p0 in range(0, len(tiles), 2):
                    sb = st_rr3[0]
                    st_rr3[0] = (sb + 1) % 3
                    pb = pt_rr3[0]
                    pt_rr3[0] = (pb + 1) % 4
                    wend = 0
                    for t_i in range(2):
                        r, kb = tiles[p0 + t_i]
                        base = r + dil * 128 * kb
                        nq = 256 if kb + 1 < nb else 128
                        c0 = t_i * 256
                        wend = c0 + nq
                        if di == 0:
                            k_ap = kh[hb][:, base:base + 128]
                            q_ap = qh[hb][:, base:base + nq]
                            rd_ = [("kh", hb), ("qh", hb), ("qz", hb), ("kz", hb)]
                        else:
                            p0_ = r * (S // dil) + 128 * kb
                            k_ap = qkp[di][1][:, p0_:p0_ + 128]
                            q_ap = qkp[di][0][:, p0_:p0_ + nq]
                            rd_ = [("qkp", di, 0), ("qkp", di, 1), ("qkpz", di, 0), ("qkpz", di, 1)]
                        mm(banks[sb][:, c0:c0 + nq], k_ap, q_ap, True, False, rd_, [("ps", sb)])
                        mm(banks[sb][:, c0:c0 + nq], identb, btab[hb][:, di, 0:nq], False, True,
                           ["ident", ("btab", hb)], [("ps", sb)])
                        pend.append((r, kb, pb, c0))
                    act(PTd[pb][:, 0:wend], banks[sb][:, 0:wend], AF.Exp, [("ps", sb)], [("PTd", pb)], scale=0.125)
                    while len(pend) > 2:
                        do_pv(pend.pop(0))
                    if norm_q:
                        norm_q.pop(0)()
                while pend:
                    do_pv(pend.pop(0))

            norm_q = []

            def norm_ops(h, q4):
                hb = h % 2
                po = (h % 2) * 64
                cs = slice(q4 * 1024, (q4 + 1) * 1024)
                return [
                    lambda: cp("dve", rect[0:64, :], acc[hb][64:128, cs], [("acc", hb)], ["rect"]),
                    lambda: act(rect[0:64, :], rect[0:64, :], AF.Ln, ["rect"], ["rect"]),
                    lambda: act(rect[0:64, :], rect[0:64, :], AF.Exp, ["rect"], ["rect"], scale=-1.0),
                    lambda: tt("dve", attnT[po:po + 64, 4 + h // 2, cs], acc[hb][0:64, cs], rect[0:64, :], ALU.mult,
                               [("acc", hb), "rect"], [("attnD", h, q4)]),
                ]

            load_qk(0)
            load_btab(0)
            for di in range(3):
                load_v(0, di)
            load_perm(0, 1)
            load_perm(0, 2)
            for h in range(8):
                hb = h % 2
                memset("pool", acc[hb], 0.0, [("acc", hb)])
                if h + 1 < 8:
                    load_qk(h + 1)
                    load_btab(h + 1)
                for di in range(3):
                    dil_tiles(h, di)
                    if h + 1 < 8:
                        load_v(h + 1, di)
                        if di > 0:
                            load_perm(h + 1, di)
                for q4 in range(4):
                    norm_q.extend(norm_ops(h, q4))
            while norm_q:
                norm_q.pop(0)()
            dump("attn_all", attnT, [128, 8, S], BF16, [("attnD", h, q) for h in range(8) for q in range(4)])
            sch.flush()

        x1T = carve(A0, [8, S], BF16)

        def layer_norm(xr_ap, key, g_ap, b_ap, sm, smk):
            st6, mv, lnv, rs, nmr = sm
            for hf in range(2):
                sch.add("dve", lambda e, hf=hf: e.bn_stats(out=st6[:, hf, :], in_=xr_ap[:, hf * 512:(hf + 1) * 512]),
                        [key], [(smk, "st", hf)], strict=True)
            sch.add("dve", lambda e: e.bn_aggr(out=mv, in_=st6.rearrange("p a b -> p (a b)")),
                    [(smk, "st", 0), (smk, "st", 1)], [(smk, "mv")], strict=True)
            act(lnv, mv[:, 1:2], AF.Ln, [(smk, "mv"), "epsl"], [(smk, "lnv")], scale=1.0, bias=epsl, strict=True)
            act(rs, lnv, AF.Exp, [(smk, "lnv")], [(smk, "rs")], scale=-0.5, strict=True)
            ts("dve", nmr, mv[:, 0:1], rs, -1.0, ALU.mult, ALU.mult, [(smk, "mv"), (smk, "rs")], [(smk, "nmr")],
               strict=True)
            act(xr_ap, xr_ap, AF.Identity, [key, (smk, "rs"), (smk, "nmr")], [key], scale=rs, bias=nmr)
            tt("dve", xr_ap, xr_ap, g_ap, ALU.mult, [key, "lnp"], [key])
            tt("pool", xr_ap, xr_ap, b_ap, ALU.add, [key, "lnp"], [key])

        if stop_after >= 4:
            c = Bump(C0, CTOP)
            wob = c([8, 1024], BF16)
            lnp1 = c([2, 1024], F32)
            NXR = 8
            xr3 = [c([1024], F32) for _ in range(NXR)]
            yb = [c([1024], BF16) for _ in range(2)]
            NSM = 4
            sms = [(c([2, 6], F32), c([2], F32), c([1], F32), c([1], F32), c([1], F32)) for _ in range(NSM)]
            wov = wo_d.rearrange("(k p) n -> p k n", p=128)
            for k in range(0, 8, 4):
                dma("pool", wob[:, k:k + 4, :], wov[:, k:k + 4, :], (), [("wob", k // 4)])
            dma("sp", lnp1, lnp_d[0:2].rearrange("a p n -> p a n"), (), ["lnp"])

            def ln_stats(xr, key, sm, smk):
                st6, mv, lnv, rs, nmr = sm
                for hf in range(2):
                    sch.add("dve", lambda e, hf=hf: e.bn_stats(out=st6[:, hf, :], in_=xr[:, hf * 512:(hf + 1) * 512]),
                            [key], [(smk, "st", hf)], strict=True)
                sch.add("dve", lambda e: e.bn_aggr(out=mv, in_=st6.rearrange("p a b -> p (a b)")),
                        [(smk, "st", 0), (smk, "st", 1)], [(smk, "mv")], strict=True)

            def ln_rstd(sm, smk):
                st6, mv, lnv, rs, nmr = sm
                act(lnv, mv[:, 1:2], AF.Ln, [(smk, "mv"), "epsl"], [(smk, "lnv")], scale=1.0, bias=epsl, strict=True)
                act(rs, lnv, AF.Exp, [(smk, "lnv")], [(smk, "rs")], scale=-0.5, strict=True)
                ts("dve", nmr, mv[:, 0:1], rs, -1.0, ALU.mult, ALU.mult, [(smk, "mv"), (smk, "rs")], [(smk, "nmr")],
                   strict=True)

            def ln_norm(xr, key, sm, smk):
                st6, mv, lnv, rs, nmr = sm
                act(xr, xr, AF.Identity, [key, (smk, "rs"), (smk, "nmr")], [key], scale=rs, bias=nmr)

            def ln_affine(xr, key, g_ap, b_ap):
                tt("dve", xr, xr, g_ap, ALU.mult, [key, "lnp"], [key])
                tt("pool", xr, xr, b_ap, ALU.add, [key, "lnp"], [key])

            def s0(i):
                tk = slice(i * 128, (i + 1) * 128)
                pb = (i % 3) * 2
                dma("sp", xr3[i % NXR], x_d[tk, :], (), [("xr", i % NXR)])
                for hf in range(2):
                    for cc in range(8):
                        mm(banks[pb + hf], attnT[:, cc, tk], wob[:, cc, hf * 512:(hf + 1) * 512],
                           cc == 0, cc == 7, ["attnT", ("wob", 0), ("wob", 1)], [("ps", pb + hf)])

            def s1(i):
                pb = (i % 3) * 2
                key = ("xr", i % NXR)
                xr = xr3[i % NXR]
                for hf in range(2):
                    hs = slice(hf * 512, (hf + 1) * 512)
                    stt("dve", xr[:, hs], xr[:, hs], DN_ALPHA, banks[pb + hf], ALU.mult, ALU.add,
                        [key, ("ps", pb + hf)], [key])
                ln_stats(xr, key, sms[i % NSM], ("sm", i % NSM))

            def s2(i):
                ln_rstd(sms[i % NSM], ("sm", i % NSM))

            def s3(i):
                ln_norm(xr3[i % NXR], ("xr", i % NXR), sms[i % NSM], ("sm", i % NSM))

            def s4(i):
                ln_affine(xr3[i % NXR], ("xr", i % NXR), lnp1[:, 0, :], lnp1[:, 1, :])

            def s5(i):
                tk = slice(i * 128, (i + 1) * 128)
                key = ("xr", i % NXR)
                dma("sp", x1_s[tk, :], xr3[i % NXR], [key], [("x1s", i)])
                act(yb[i % 2], xr3[i % NXR], AF.Copy, [key], [("yb", i % 2)])

            def s6(i):
                tk = slice(i * 128, (i + 1) * 128)
                tb = 6 + (i % 2)
                pT = banks[tb].bitcast(BF16).rearrange("p (a b) -> p a b", a=8, b=128)
                for cc in range(8):
                    tr(pT[:, cc, :], yb[i % 2][:, cc * 128:(cc + 1) * 128], identb, [("yb", i % 2), "ident"], [("ps", tb)])
                cp("act", x1T[:, :, tk], pT, [("ps", tb)], [("x1T", i)])

            stages = [(s6, 7), (s5, 6), (s4, 5), (s3, 4), (s2, 3), (s1, 2), (s0, 0)]
            for t in range(32 + 7):
                for fn_, lag in stages:
                    i = t - lag
                    if 0 <= i < 32:
                        fn_(i)
            dump("x1T", x1T, [128, 8, S], BF16, [("x1T", i) for i in range(32)])
            sch.flush()

        if stop_after >= 5:
            bb = Bump(B0, C0)
            c = Bump(C0, CTOP)
            hmid = bb([NJ, 1024], BF16)
            wupb = [bb([8, 256], BF16) for _ in range(3)]
            yy = [[bb([1024], F32), c([1024], F32)] for _ in range(2)]
            sms = [(c([2, 6], F32), c([2], F32), c([1], F32), c([1], F32), c([1], F32)) for _ in range(2)]
            wdb = c([NJ, 1024], BF16)
            lnp2 = c([2, 1024], F32)
            cw = c([44, 4], F32)
            halo = c([44, 2], F32)
            xr2 = [c([1024], F32) for _ in range(2)]
            dma("sp", lnp2, lnp_d[2:4].rearrange("a p n -> p a n"), (), ["lnp"])
            dma("sp", cw, cw_d.rearrange("p (a b) -> p a b", a=44, b=4), (), ["cw"])
            memset("pool", halo, 0.0, ["halo"])
            wdv = wd_d.rearrange("(j p) n -> p j n", p=128)
            wupv = wup_d.rearrange("j (k p) n -> j p k n", p=128)

            def load_wd():
                for j0 in range(0, NJ, 2):
                    dma("pool", wdb[:, j0:j0 + 2, :], wdv[:, j0:j0 + 2, :], (), [("wdb", j0 // 2)])

            seq = [(s, j) for s in range(4) for j in range(NJ)]

            def load_wup(it):
                dma("pool", wupb[it % 3], wupv[seq[it][1]], (), [("wupb", it % 3)])

            def up_mm(it):
                s, j = seq[it]
                q = it % 2
                for hf in range(2):
                    for t2 in range(2):
                        b = 4 * q + 2 * hf + t2
                        tok0 = s * 1024 + t2 * 512
                        for k in range(8):
                            mm(banks[b], wupb[it % 3][:, k, hf * 128:(hf + 1) * 128], x1T[:, k, tok0:tok0 + 512],
                               k == 0, k == 7, [("wupb", it % 3), "x1T"], [("ps", b)])

            def up_conv(it):
                s, j = seq[it]
                q = it % 2
                for hf in range(2):
                    ch = j + NJ * hf
                    b0 = 4 * q + 2 * hf
                    U = psum[:, b0 * 512:(b0 + 2) * 512]
                    ur = [("ps", b0), ("ps", b0 + 1)]
                    y = yy[hf][q]
                    yk = ("yy", hf, q)
                    hk = ("halo", ch)
                    act(y, U, AF.Identity, ur + ["cw"], [yk], scale=cw[:, ch, 2:3], bias=cw[:, ch, 3:4])
                    stt("dve", y[:, 1:1024], U[:, 0:1023], cw[:, ch, 1:2], y[:, 1:1024], ALU.mult, ALU.add,
                        ur + ["cw", yk], [yk])
                    stt("dve", y[:, 2:1024], U[:, 0:1022], cw[:, ch, 0:1], y[:, 2:1024], ALU.mult, ALU.add,
                        ur + ["cw", yk], [yk])
                    stt("dve", y[:, 0:2], halo[:, ch, 0:2], cw[:, ch, 0:1], y[:, 0:2], ALU.mult, ALU.add,
                        [hk, "halo", "cw", yk], [yk], strict=True)
                    stt("dve", y[:, 0:1], halo[:, ch, 1:2], cw[:, ch, 1:2], y[:, 0:1], ALU.mult, ALU.add,
                        [hk, "halo", "cw", yk], [yk], strict=True)
                    cp("dve", halo[:, ch, :], U[:, 1022:1024], ur, [hk], strict=True)

            def up_gate(it):
                s, j = seq[it]
                q = it % 2
                act(yy[1][q], yy[1][q], AF.Gelu_apprx_tanh, [("yy", 1, q)], [("yy", 1, q)])
                tt("pool", hmid[:, j, :], yy[1][q], yy[0][q], ALU.mult, [("yy", 1, q), ("yy", 0, q)], [("hmid", j)])

            def load_x1(n):
                tk0 = (n // 8) * 1024 + (n % 8) * 128
                dma("sp", xr2[n % 2], x1_s[tk0:tk0 + 128, :], [("x1s", n)], [("xr", n % 2)])

            n_dn = [0]

            def down(s, sub):
                tk0 = s * 1024 + sub * 128
                tk = slice(tk0, tk0 + 128)
                n = s * 8 + sub
                xb2 = n % 2
                key = ("xr", xb2)
                pb = (n_dn[0] % 4) * 2
                n_dn[0] += 1
                for hf in range(2):
                    for j in range(NJ):
                        mm(banks[pb + hf], hmid[:, j, sub * 128:(sub + 1) * 128], wdb[:, j, hf * 512:(hf + 1) * 512],
                           j == 0, j == NJ - 1, [("hmid", j), ("wdb", j // 2)], [("ps", pb + hf)])
                if n + 1 < 32:
                    load_x1(n + 1)
                for hf in range(2):
                    hs = slice(hf * 512, (hf + 1) * 512)
                    stt("dve", xr2[xb2][:, hs], xr2[xb2][:, hs], DN_ALPHA, banks[pb + hf], ALU.mult, ALU.add,
                        [key, ("ps", pb + hf)], [key])
                layer_norm(xr2[xb2], key, lnp2[:, 0, :], lnp2[:, 1, :], sms[n % 2], ("sm", n % 2))
                dma("sp", out_d[tk, :], xr2[xb2], [key], [("out", n)])

            load_wup(0)
            load_wup(1)
            load_x1(0)
            for it in range(len(seq)):
                s, j = seq[it]
                if it + 2 < len(seq):
                    load_wup(it + 2)
                if it == 2:
                    load_wd()
                up_mm(it)
                up_conv(it)
                if j > 0:
                    up_gate(it - 1)
                if j == NJ - 1:
                    up_gate(it)
                    for sub in range(8):
                        down(s, sub)
            sch.flush()
        else:
            sch.flush()
    build.last_log = sch.log
    return nc, dump_out


def _tables():
    f32 = np.float32
    half = 16
    freqs = np.power(f32(10000.0), -(np.arange(half, dtype=f32) / f32(half))).astype(f32)
    ang = (np.arange(S, dtype=f32)[:, None] * freqs[None, :]).astype(f32)
    cos = np.cos(ang).astype(f32).T
    sin = np.sin(ang).astype(f32).T
    cc = np.concatenate([cos, cos], 0)
    ss = np.concatenate([-sin, sin], 0)
    qtab = np.concatenate([np.ones((64, S), f32), cc, ss], 0)
    z = np.zeros((64, S), f32)
    ktab = np.stack([np.concatenate([z, cc, cc], 0), np.concatenate([z, ss, ss], 0)], 0)
    kk = np.arange(128)[:, None]
    qq = np.arange(128)[None, :]
    tri = np.where(kk <= qq, 0.0, -30000.0).astype(f32)
    slopes = 2.0 ** (-8.0 * np.arange(1, 9, dtype=np.float64) / 8.0)
    ql = np.arange(256)[None, :]
    off = ql - kk
    valid = (off >= 0) & (off <= 128)
    mt = np.zeros((128, 24, 256), f32)
    for h in range(8):
        for di, dil in enumerate(DILS):
            bias = -8.0 * slopes[h] * dil * off.astype(np.float64)
            mt[:, h * 3 + di, :] = np.where(valid, bias, -240000.0).astype(f32)
    return qtab, ktab, tri, mt.reshape(128, 24 * 256)


def _prep_shared(inp):
    f32 = np.float32
    w_in = np.asarray(inp["w_in"], f32)
    kr = w_in[:, 384:416]
    krs = np.concatenate([kr[:, 16:32], kr[:, 0:16]], 1)
    w1 = np.concatenate([w_in[:, 0:384], kr, kr, kr, kr, krs, krs, krs, krs, w_in[:, 416:1952]], 1)
    assert w1.shape == (D, 2176)
    w_uq = np.asarray(inp["w_uq"], f32)
    rope = w_uq[:, :, 64:96]
    swap = np.concatenate([rope[:, :, 16:32], rope[:, :, 0:16]], 2)
    wq = np.concatenate([w_uq, swap], 2).reshape(256, 1024)
    gq = np.zeros((128, 4), f32)
    gq[:, 0:2] = np.asarray(inp["g_cq"], f32).reshape(2, 128).T
    gq[:, 2] = np.asarray(inp["g_ckv"], f32)
    qtab, ktab, tri, mtab = _tables()
    w_up = np.asarray(inp["w_up"], f32)
    wup = np.stack([np.concatenate([w_up[:, j * 128:(j + 1) * 128], w_up[:, DFF + j * 128:DFF + (j + 1) * 128]], 1)
                    for j in range(NJ)], 0)
    conv_w = np.asarray(inp["conv_w"], f32)
    conv_b = np.asarray(inp["conv_b"], f32)
    cw = np.concatenate([conv_w, conv_b[None, :]], 0)
    cw = cw.reshape(4, 44, 128).transpose(2, 1, 0).reshape(128, 44 * 4)
    lnp = np.stack([np.broadcast_to(np.asarray(inp[k], f32)[None, :], (128, D))
                    for k in ("ln1_g", "ln1_b", "ln2_g", "ln2_b")], 0)
    c = np.ascontiguousarray
    return {
        "w1": c(w1), "wq": c(wq), "wuk": c(np.asarray(inp["w_uk"], f32).reshape(128, 512)),
        "wuv": c(np.asarray(inp["w_uv"], f32).reshape(128, 512)), "gq": gq, "qtab": c(qtab), "ktab": c(ktab),
        "tri": c(tri), "mtab": c(mtab), "wo": c(np.asarray(inp["w_o"], f32)), "wup": c(wup), "cw": c(cw),
        "wd": c(np.asarray(inp["w_down"], f32)), "lnp": c(lnp), "ident": np.eye(128, dtype=f32),
    }


def kernel(**inputs):
    x = np.asarray(inputs["x"], np.float32)
    B = x.shape[0]
    shared = _prep_shared(inputs)
    nc, _ = build()
    in_maps = []
    for b in range(B):
        m = dict(shared)
        m["x"] = np.ascontiguousarray(x[b])
        m["xT"] = np.ascontiguousarray(x[b].T)
        in_maps.append(m)
    res = run_bass_kernel_spmd(nc, in_maps, core_ids=list(range(B)))
    return np.stack([np.asarray(r["out"], np.float32) for r in res.results], 0)
```

```python
import math
from contextlib import ExitStack

import numpy as np
import concourse.bass as bass
import concourse.mybir as mybir
from concourse.bass_utils import run_bass_kernel_spmd

F32 = mybir.dt.float32
BF16 = mybir.dt.bfloat16
ALU = mybir.AluOpType
AF = mybir.ActivationFunctionType

S = 4096
D = 1024
DFF = 2816
NJ = 22
DN_ALPHA = 2.0 ** 0.25
LN_EPS = 1e-5
RMS_EPS = 1e-6
DILS = (1, 4, 16)
KB = 1024

SAME_ENG_SYNC = True
ENGS = ("pe", "act", "dve", "pool", "sp")


class _Op:
    __slots__ = ("eng", "fn", "deps", "dma", "signal", "sem", "val", "prev", "strict")


class Sched:
    def __init__(self, nc, stack, n_dma=36):
        self.nc = nc
        self.ops = []
        self.flushed = 0
        self.last_w = {}
        self.rd_eng = {}
        self.rd_dma = {}
        self.sems = {}
        for e in ENGS:
            self.sems[("e", e)] = stack.enter_context(nc.semaphore("s_" + e))
        for i in range(n_dma):
            self.sems[("d", i)] = stack.enter_context(nc.semaphore("d%d" % i))
        self.n_dma = n_dma
        self.eng_cnt = {e: 0 for e in ENGS}
        self.dma_val = [0] * n_dma
        self.dma_rr = 0
        self.sw_rr = 0
        self.n_hw = 20
        self.known = {e: {} for e in ENGS}
        self.log = {}

    def add(self, eng, fn, reads=(), writes=(), dma=False, strict=False):
        i = len(self.ops)
        if eng != "pe":
            writes = list(writes) + [r for r in reads if isinstance(r, tuple) and r[0] == "ps" and r not in writes]
        deps = set()
        for r in reads:
            if r in self.last_w:
                deps.add(self.last_w[r])
        for w in writes:
            if w in self.last_w:
                deps.add(self.last_w[w])
            deps.update(self.rd_eng.get(w, {}).values())
            deps.update(self.rd_dma.get(w, ()))
        for r in reads:
            if dma:
                self.rd_dma.setdefault(r, []).append(i)
            else:
                self.rd_eng.setdefault(r, {})[eng] = i
        for w in writes:
            self.last_w[w] = i
            self.rd_eng[w] = {}
            self.rd_dma[w] = []
        op = _Op()
        op.eng, op.fn, op.deps, op.dma = eng, fn, deps, dma
        op.signal, op.sem, op.val, op.prev = False, None, 0, 0
        op.strict = strict
        self.ops.append(op)
        return i

    def _skip(self, p, op):
        if p.dma or op.dma:
            return False
        if p.eng != op.eng:
            return False
        if p.eng == "pe":
            return True
        return not (SAME_ENG_SYNC or p.strict or op.strict)

    def flush(self):
        nc = self.nc
        ops = self.ops[self.flushed:]
        self.flushed = len(self.ops)
        for op in ops:
            for d in op.deps:
                p = self.ops[d]
                if not p.dma and not self._skip(p, op):
                    p.signal = True
        last = {}
        for op in ops:
            if not op.dma:
                last[op.eng] = op
        for op in last.values():
            op.signal = True
        for op in ops:
            if op.dma:
                if op.eng == "pool":
                    k = self.n_hw + self.sw_rr
                    self.sw_rr = (self.sw_rr + 1) % (self.n_dma - self.n_hw)
                else:
                    k = self.dma_rr
                    self.dma_rr = (k + 1) % self.n_hw
                op.sem = ("d", k)
                op.prev = self.dma_val[k]
                self.dma_val[k] += 16
                op.val = self.dma_val[k]
            elif op.signal:
                self.eng_cnt[op.eng] += 1
                op.sem = ("e", op.eng)
                op.val = self.eng_cnt[op.eng]

        def emit(name, e):
            known = self.known[name]

            def wait(key, val):
                if val <= 0 or known.get(key, 0) >= val:
                    return
                e.wait_ge(self.sems[key], val)
                known[key] = val
                self.log.setdefault(name, []).append(("w", key, val))

            for op in ops:
                if op.eng != name:
                    continue
                need = {}
                for d in op.deps:
                    p = self.ops[d]
                    if p.sem is None or self._skip(p, op):
                        continue
                    need[p.sem] = max(need.get(p.sem, 0), p.val)
                if op.dma and op.prev > 0:
                    need[op.sem] = max(need.get(op.sem, 0), op.prev)
                for k, v in need.items():
                    wait(k, v)
                ins = op.fn(e)
                if op.dma:
                    ins.then_inc(self.sems[op.sem], 16)
                    self.log.setdefault(name, []).append(("i", op.sem, 16))
                elif op.signal:
                    ins.then_inc(self.sems[op.sem], 1)
                    self.log.setdefault(name, []).append(("i", op.sem, 1))
            for en in ENGS:
                if en != name:
                    wait(("e", en), self.eng_cnt[en])
            for k in range(self.n_dma):
                wait(("d", k), self.dma_val[k])

        with nc.Block() as block:
            @block.tensor
            def _(e):
                emit("pe", e)

            @block.scalar
            def _(e):
                emit("act", e)

            @block.vector
            def _(e):
                emit("dve", e)

            @block.gpsimd
            def _(e):
                emit("pool", e)

            @block.sync
            def _(e):
                emit("sp", e)

        self.last_w.clear()
        self.rd_eng.clear()
        self.rd_dma.clear()


def _dsize(dtype):
    return 4 if dtype == F32 else 2


def build(stop_after=5, dumps=()):
    nc = bass.Bass("TRN2", target_bir_lowering=False)

    def din(name, shape, dtype=F32):
        return nc.dram_tensor(name, shape, dtype, kind="ExternalInput").ap()

    xT_d = din("xT", [D, S])
    x_d = din("x", [S, D])
    w1_d = din("w1", [D, 2176])
    wq_d = din("wq", [256, 1024])
    wuk_d = din("wuk", [128, 512])
    wuv_d = din("wuv", [128, 512])
    gq_d = din("gq", [128, 4])
    qtab_d = din("qtab", [128, S])
    ktab_d = din("ktab", [2, 128, S])
    tri_d = din("tri", [128, 128])
    mtab_d = din("mtab", [128, 24 * 256])
    wo_d = din("wo", [D, D])
    wup_d = din("wup", [NJ, D, 256])
    cw_d = din("cw", [128, 44 * 4])
    wd_d = din("wd", [DFF, D])
    lnp_d = din("lnp", [4, 128, D])
    ident_d = din("ident", [128, 128])
    out_d = nc.dram_tensor("out", [S, D], F32, kind="ExternalOutput").ap()
    qd_s = nc.dram_tensor("qd_s", [512, S], BF16, kind="Internal").ap()
    kd_s = nc.dram_tensor("kd_s", [512, S], BF16, kind="Internal").ap()
    qk4_s = [nc.dram_tensor(n_, [512, S], BF16, kind="Internal").ap() for n_ in ("qd4_s", "kd4_s")]
    qk16_s = [nc.dram_tensor(n_, [512, S], BF16, kind="Internal").ap() for n_ in ("qd16_s", "kd16_s")]
    vd_s = nc.dram_tensor("vd_s", [S, 512], BF16, kind="Internal").ap()
    x1_s = nc.dram_tensor("x1_s", [S, D], F32, kind="Internal").ap()

    dump_out = {}

    with ExitStack() as stack:
        ARENA_BYTES = 206 * KB
        arena = stack.enter_context(nc.sbuf_tensor("arena", [128, ARENA_BYTES // 2], BF16))
        psum = stack.enter_context(nc.psum_tensor("psum", [128, 4096], F32))
        banks = [psum[:, i * 512:(i + 1) * 512] for i in range(8)]
        sch = Sched(nc, stack)

        def carve(off, shape, dtype):
            n = 1
            for v in shape:
                n *= v
            nb = n * _dsize(dtype)
            assert off % 4 == 0 and off + nb <= ARENA_BYTES, (off, nb)
            ap = arena[:, off // 2:(off + nb) // 2]
            if dtype != BF16:
                ap = ap.bitcast(dtype)
            if len(shape) == 2:
                ap = ap.rearrange("p (a b) -> p a b", a=shape[0], b=shape[1])
            elif len(shape) == 3:
                ap = ap.rearrange("p (a b c) -> p a b c", a=shape[0], b=shape[1], c=shape[2])
            return ap

        class Bump:
            def __init__(self, start, end):
                self.off, self.end = start, end

            def __call__(self, shape, dtype):
                n = 1
                for v in shape:
                    n *= v
                nb = (n * _dsize(dtype) + 63) // 64 * 64
                ap = carve(self.off, shape, dtype)
                self.off += nb
                assert self.off <= self.end, (self.off, self.end)
                return ap

        A0, B0, C0, CEND = 0, 64 * KB, 128 * KB, ARENA_BYTES

        def mm(out, lhsT, rhs, start, stop, reads, writes):
            sch.add("pe", lambda e: e.matmul(out, lhsT=lhsT, rhs=rhs, start=start, stop=stop), reads, writes)

        def tr(out, in_, ident, reads, writes):
            sch.add("pe", lambda e: e.transpose(out, in_, ident), reads, writes)

        def act(out, in_, func, reads, writes, scale=None, bias=None, strict=False):
            kw = {}
            if scale is not None:
                kw["scale"] = scale
            if bias is not None:
                kw["bias"] = bias
            sch.add("act", lambda e: e.activation(out=out, in_=in_, func=func, **kw), reads, writes, strict=strict)

        def tt(eng, out, in0, in1, op, reads, writes, strict=False):
            sch.add(eng, lambda e: e.tensor_tensor(out=out, in0=in0, in1=in1, op=op), reads, writes, strict=strict)

        def stt(eng, out, in0, scalar, in1, op0, op1, reads, writes, strict=False):
            sch.add(eng, lambda e: e.scalar_tensor_tensor(out=out, in0=in0, scalar=scalar, in1=in1,
                                                          op0=op0, op1=op1), reads, writes, strict=strict)

        def ts(eng, out, in0, s1, s2, op0, op1, reads, writes, strict=False):
            sch.add(eng, lambda e: e.tensor_scalar(out=out, in0=in0, scalar1=s1, scalar2=s2, op0=op0, op1=op1),
                    reads, writes, strict=strict)

        def cp(eng, out, in_, reads, writes, strict=False):
            if eng == "act":
                act(out, in_, AF.Copy, reads, writes, strict=strict)
            else:
                sch.add(eng, lambda e: e.tensor_copy(out=out, in_=in_), reads, writes, strict=strict)

        def recip(out, in_, reads, writes):
            sch.add("dve", lambda e: e.reciprocal(out=out, in_=in_), reads, writes, strict=True)

        def memset(eng, ap, val, writes):
            sch.add(eng, lambda e: e.memset(ap, val), (), writes)

        def dma(eng, out, in_, reads, writes):
            sch.add(eng, lambda e: e.dma_start(out=out, in_=in_), reads, writes, dma=True)

        def dump(name, ap, shape, dtype, reads):
            if name not in dumps:
                return
            o = nc.dram_tensor("dbg_" + name, shape, dtype, kind="ExternalOutput").ap()
            dump_out[name] = o
            dma("sp", o, ap, reads, [("dbg", name)])

        evac_rr = [0]

        def evac(out, in_, reads, writes):
            evac_rr[0] ^= 1
            cp("act" if evac_rr[0] else "dve", out, in_, reads, writes)

        cst = Bump(CEND - 2 * KB, CEND)
        epsr = cst([1], F32)
        epsl = cst([1], F32)
        onesf = cst([128], F32)
        identb = cst([128], BF16)
        memset("pool", epsr, RMS_EPS, ["epsr"])
        memset("pool", epsl, LN_EPS, ["epsl"])
        memset("pool", onesf, 1.0, ["onesf"])
        dma("pool", identb, ident_d, (), ["ident"])
        CTOP = CEND - 2 * KB

        cq = carve(C0, [2, S], BF16)
        ckv = carve(C0 + 16 * KB, [S], BF16)
        krd = carve(C0 + 24 * KB, [S], BF16)

        a = Bump(A0, B0)
        c = Bump(C0 + 32 * KB, CTOP)
        w1b = a([8, 2176], BF16)
        xt = [a([8, 512], BF16) for _ in range(2)]
        stg = [a([512], BF16) for _ in range(4)]
        stg4 = [a([512], BF16) for _ in range(4)]
        P16 = carve(B0, [8, S], BF16)
        s4_rr = [0]
        gq = c([4], F32)
        raw = c([3, 512], F32)
        sq = c([3, 512], F32)
        rstd = c([2, 512], F32)
        lnt = c([2, 512], F32)
        ktab = [c([2, 512], F32) for _ in range(2)]
        t1 = c([512], F32)
        t2 = c([512], F32)

        w1v = w1_d.rearrange("(k p) n -> p k n", p=128)
        xTv = xT_d.rearrange("(k p) t -> p k t", p=128)
        for k in range(0, 8, 4):
            dma("pool", xt[0][:, k:k + 4, :], xTv[:, k:k + 4, 0:512], (), [("xt", 0, k // 4)])
        for k in range(8):
            dma("pool", w1b[:, k, 640:1664], w1v[:, k, 640:1664], (), [("w1c", k)])
        for k in range(8):
            dma("pool", w1b[:, k, 0:640], w1v[:, k, 0:640], (), [("w1b", k)])
        for k in range(8):
            dma("pool", w1b[:, k, 1664:2176], w1v[:, k, 1664:2176], (), [("w1v", k)])
        dma("sp", gq, gq_d, (), ["gq"])
        ktv = ktab_d.rearrange("a p t -> p a t")
        w1r = [("w1b", k) for k in range(8)]
        bank_rr = [0]
        stg_rr = [0]

        def next_bank(n=6):
            b = bank_rr[0]
            bank_rr[0] = (b + 1) % n
            return b

        def store_p16(hf_):
            for g in range(8):
                r0 = (g % 4) * 128
                dv = qk16_s[g // 4][r0:r0 + 128, :].rearrange("p (r j) -> p r j", r=16)
                sv = P16[:, g, :].rearrange("p (r j) -> p r j", r=16)
                for rh in range(2):
                    dma("sp", dv[:, rh * 8:(rh + 1) * 8, hf_ * 128:(hf_ + 1) * 128],
                        sv[:, rh * 8:(rh + 1) * 8, hf_ * 128:(hf_ + 1) * 128],
                        [("P16", g, tt_) for tt_ in range(hf_ * 4, hf_ * 4 + 4)], [("qkd16", g, hf_, rh)])

        for t in range(8):
            xb = t % 2
            tok = slice(t * 512, (t + 1) * 512)
            if t > 0:
                for k in range(0, 8, 4):
                    dma("pool", xt[xb][:, k:k + 4, :], xTv[:, k:k + 4, tok], (), [("xt", xb, k // 4)])
            dma("sp", ktab[xb][64:128], ktv[64:128, :, tok], (), [("ktab", xb)])
            xr = [("xt", xb, 0), ("xt", xb, 1)]
            for gi in list(range(5, 13)) + list(range(5)):
                if gi == 0 and t % 4 == 3:
                    store_p16(t // 4)
                b = next_bank()
                ps = banks[b]
                for k in range(8):
                    mm(ps[:, :], w1b[:, k, gi * 128:(gi + 1) * 128], xt[xb][:, k, :], k == 0, k == 7,
                       xr + [("w1b" if gi < 5 else "w1c", k)], [("ps", b)])
                if gi < 3:
                    act(raw[:, gi, :], ps[:, :], AF.Copy, [("ps", b)], [("raw", gi)])
                    act(sq[:, gi, :], ps[:, :], AF.Square, [("ps", b)], [("sq", gi)])
                    if gi == 1:
                        mm(banks[6][:, :], onesf, sq[:, 0, :], True, False, ["onesf", ("sq", 0)], [("ps", 6)])
                        mm(banks[6][:, :], onesf, sq[:, 1, :], False, True, ["onesf", ("sq", 1)], [("ps", 6)])
                        act(lnt[:, 0, :], banks[6][:, :], AF.Ln, [("ps", 6), "epsr"], [("lnt", 0)],
                            scale=1.0 / 256, bias=epsr)
                        act(rstd[:, 0, :], lnt[:, 0, :], AF.Exp, [("lnt", 0)], [("rstd", 0)], scale=-0.5)
                        for cc in range(2):
                            stt("dve", cq[:, cc, tok], raw[:, cc, :], gq[:, cc:cc + 1], rstd[:, 0, :],
                                ALU.mult, ALU.mult, [("raw", cc), "gq", ("rstd", 0)], [("cq", t)])
                    if gi == 2:
                        mm(banks[7][:, :], onesf, sq[:, 2, :], True, True, ["onesf", ("sq", 2)], [("ps", 7)])
                        act(lnt[:, 1, :], banks[7][:, :], AF.Ln, [("ps", 7), "epsr"], [("lnt", 1)],
                            scale=1.0 / 128, bias=epsr)
                        act(rstd[:, 1, :], lnt[:, 1, :], AF.Exp, [("lnt", 1)], [("rstd", 1)], scale=-0.5)
                        stt("dve", ckv[:, tok], raw[:, 2, :], gq[:, 2:3], rstd[:, 1, :],
                            ALU.mult, ALU.mult, [("raw", 2), "gq", ("rstd", 1)], [("ckv", t)])
                elif gi == 3:
                    tt("dve", t1[64:128, :], ps[64:128, :], ktab[xb][64:128, 0, :], ALU.mult,
                       [("ps", b), ("ktab", xb)], ["t1"])
                elif gi == 4:
                    tt("dve", t2[64:128, :], ps[64:128, :], ktab[xb][64:128, 1, :], ALU.mult,
                       [("ps", b), ("ktab", xb)], ["t2"])
                    tt("dve", krd[64:128, tok], t1[64:128, :], t2[64:128, :], ALU.add, ["t1", "t2"], [("krd", t)])
                else:
                    si = stg_rr[0]
                    stg_rr[0] = (si + 1) % 4
                    g = gi - 5
                    eng_ = "act" if g % 2 == 0 else "dve"
                    cp(eng_, stg[si], ps[:, :], [("ps", b)], [("stg", si)])
                    dst = qd_s if gi < 9 else kd_s
                    r0 = (g % 4) * 128
                    dma("sp", dst[r0:r0 + 128, tok], stg[si], [("stg", si)], [("qkd", gi, t)])
                    s4 = s4_rr[0]
                    s4_rr[0] = (s4 + 1) % 4
                    cp(eng_, stg4[s4].rearrange("p (r j) -> p r j", r=4), ps.rearrange("p (j r) -> p r j", r=4),
                       [("ps", b)], [("stg4", s4)])
                    d4 = qk4_s[0 if gi < 9 else 1][r0:r0 + 128, :].rearrange("p (r j) -> p r j", r=4)
                    dma("sp", d4[:, :, t * 128:(t + 1) * 128], stg4[s4].rearrange("p (r j) -> p r j", r=4),
                        [("stg4", s4)], [("qkd4", gi, t)])
                    cp(eng_, P16[:, g, :].rearrange("p (r j) -> p r j", r=16)[:, :, t * 32:(t + 1) * 32],
                       ps.rearrange("p (j r) -> p r j", r=16), [("ps", b)], [("P16", g, t)])
            for sub in range(4):
                b = next_bank()
                ps = banks[b]
                for k in range(8):
                    mm(ps[:, :], xt[xb][:, k, sub * 128:(sub + 1) * 128], w1b[:, k, 1664:2176], k == 0, k == 7,
                       xr + [("w1v", k)], [("ps", b)])
                si = stg_rr[0]
                stg_rr[0] = (si + 1) % 4
                evac(stg[si], ps[:, :], [("ps", b)], [("stg", si)])
                r0 = t * 512 + sub * 128
                dma("sp", vd_s[r0:r0 + 128, :], stg[si], [("stg", si)], [("vd", t, sub)])
        dump("cq", cq, [128, 2, S], BF16, [("cq", t) for t in range(8)])
        dump("ckv", ckv, [128, S], BF16, [("ckv", t) for t in range(8)])
        dump("krd", krd[64:128], [64, S], BF16, [("krd", t) for t in range(8)])
        sch.flush()
        if "qd_s" in dumps:
            for nm, src in (("qd_s", qd_s), ("kd_s", kd_s)):
                o = nc.dram_tensor("dbg_" + nm, [512, S], BF16, kind="ExternalOutput").ap()
                dump_out[nm] = o
                dma("sp", o, src, (), [("dbg", nm)])
            o = nc.dram_tensor("dbg_vd_s", [S, 512], BF16, kind="ExternalOutput").ap()
            dump_out["vd_s"] = o
            dma("sp", o, vd_s, (), [("dbg", "vd_s")])
            sch.flush()

        attnT = carve(B0, [8, S], BF16)

        if stop_after >= 2:
            a = Bump(A0, B0)
            c = Bump(C0 + 32 * KB, CTOP)
            QT = [a([S], BF16) for _ in range(2)]
            KT = [a([S], BF16) for _ in range(2)]
            VH = [a([32, 128], BF16) for _ in range(2)]
            qtab = c([S], F32)
            wqb = c([2, 1024], BF16)
            wukb = c([512], BF16)
            wuvb = c([512], BF16)
            trib = c([128], BF16)
            rec = [c([512], F32) for _ in range(2)]
            PT = [c([512], BF16) for _ in range(4)]
            dma("sp", qtab, qtab_d, (), ["qtab"])
            dma("pool", wqb, wq_d.rearrange("(k p) n -> p k n", p=128), (), ["wqb"])
            dma("pool", wukb, wuk_d, (), ["wukb"])
            dma("pool", wuvb, wuv_d, (), ["wuvb"])
            dma("pool", trib, tri_d, (), ["trib"])
            for hb in range(2):
                memset("pool", VH[hb][:, :, 64:128], 1.0, [("VH1", hb)])
            sc_mla = 1.0 / math.sqrt(96.0)
            prep_rr = [0]

            def prep_units(h):
                hb = h % 2
                units = []

                def q_unit(t):
                    tok = slice(t * 512, (t + 1) * 512)
                    b = 5 + prep_rr[0]
                    prep_rr[0] ^= 1
                    for k in range(2):
                        mm(banks[b][:, :], wqb[:, k, h * 128:(h + 1) * 128], cq[:, k, tok], k == 0, k == 1,
                           ["wqb", "cq"], [("ps", b)])
                    tt("dve", QT[hb][:, tok], banks[b][:, :], qtab[:, tok], ALU.mult,
                       [("ps", b), "qtab"], [("QT", hb)])

                def k_unit(t):
                    tok = slice(t * 512, (t + 1) * 512)
                    b = 5 + prep_rr[0]
                    prep_rr[0] ^= 1
                    mm(banks[b][0:64, :], wukb[:, h * 64:(h + 1) * 64], ckv[:, tok], True, True,
                       ["wukb", "ckv"], [("ps", b)])
                    cp("dve", KT[hb][0:64, tok], banks[b][0:64, :], [("ps", b)], [("KT", hb)])

                def kr_unit():
                    cp("pool", KT[hb][64:128, :], krd[64:128, :], ["krd"], [("KT", hb)])

                def v_unit(grp):
                    for i in range(8):
                        kt = grp * 8 + i
                        mm(banks[7][:, i * 64:(i + 1) * 64], ckv[:, kt * 128:(kt + 1) * 128],
                           wuvb[:, h * 64:(h + 1) * 64], True, True, ["wuvb", "ckv"], [("ps", 7)])
                    cp("dve", VH[hb][:, grp * 8:(grp + 1) * 8, 0:64],
                       banks[7][:, :].rearrange("p (a b) -> p a b", a=8, b=64), [("ps", 7)], [("VH", hb)])

                units.append(kr_unit)
                for t in range(8):
                    units.append(lambda t=t: q_unit(t))
                    units.append(lambda t=t: k_unit(t))
                for grp in range(4):
                    units.append(lambda grp=grp: v_unit(grp))
                return units

            st_rr = [0]
            pt_rr = [0]

            def attn(h, units):
                hb = h % 2
                tiles = []
                for qc in range(8):
                    for kt in range(4 * qc + 4):
                        tiles.append((qc, kt))
                pend = []

                def do_pv(item):
                    qc, kt, n0, N, pb = item
                    ob = 3 + (qc % 2)
                    mm(banks[ob][:, n0:512], VH[hb][:, kt, :], PT[pb][:, 0:N], kt == 0, kt == 4 * qc + 3,
                       [("VH", hb), ("VH1", hb), ("PT", pb)], [("ps", ob)])
                    if kt == 4 * qc + 3:
                        rb = qc % 2
                        recip(rec[rb][0:64, :], banks[ob][64:128, :], [("ps", ob)], [("rec", rb)])
                        po = (h % 2) * 64
                        tt("dve", attnT[po:po + 64, h // 2, qc * 512:(qc + 1) * 512], banks[ob][0:64, :],
                           rec[rb][0:64, :], ALU.mult, [("ps", ob), ("rec", rb)], [("attnT", h, qc)])

                for (qc, kt) in tiles:
                    n0 = max(0, kt * 128 - qc * 512)
                    N = 512 - n0
                    sb = st_rr[0]
                    st_rr[0] = (sb + 1) % 3
                    pb = pt_rr[0]
                    pt_rr[0] = (pb + 1) % 4
                    diag = kt >= 4 * qc
                    mm(banks[sb][:, 0:N], KT[hb][:, kt * 128:(kt + 1) * 128],
                       QT[hb][:, qc * 512 + n0:(qc + 1) * 512], True, not diag,
                       [("KT", hb), ("QT", hb)], [("ps", sb)])
                    if diag:
                        mm(banks[sb][:, 0:128], identb, trib, False, True, ["ident", "trib"], [("ps", sb)])
                    act(PT[pb][:, 0:N], banks[sb][:, 0:N], AF.Exp, [("ps", sb)], [("PT", pb)], scale=sc_mla)
                    pend.append((qc, kt, n0, N, pb))
                    if len(pend) > 2:
                        do_pv(pend.pop(0))
                    n_t[0] += 1
                    if units and n_t[0] % 5 == 0:
                        units.pop(0)()
                while pend:
                    do_pv(pend.pop(0))
                while units:
                    units.pop(0)()

            n_t = [0]
            for u in prep_units(0):
                u()
            for h in range(8):
                attn(h, prep_units(h + 1) if h + 1 < 8 else [])
            dump("QT0", QT[0], [128, S], BF16, [("QT", 0)])
            dump("KT0", KT[0], [128, S], BF16, [("KT", 0)])
            dump("VH0", VH[0], [128, 32, 128], BF16, [("VH", 0)])
            dump("attn_mla", attnT[:, 0:4, :], [128, 4, S], BF16,
                 [("attnT", h, qc) for h in range(8) for qc in range(8)])
            sch.flush()

        if stop_after >= 3:
            a = Bump(A0, B0)
            c = Bump(C0, CTOP)
            qh = [a([S], BF16) for _ in range(2)]
            kh = [a([S], BF16) for _ in range(2)]
            VD = [a([32, 128], BF16) for _ in range(3)]
            rect = a([1024], F32)
            acc = [c([S], F32) for _ in range(2)]
            btab = [c([3, 256], BF16) for _ in range(2)]
            PTd = [c([512], BF16) for _ in range(4)]
            qkp = {1: (c([S], BF16), c([S], BF16)), 2: (c([S], BF16), c([S], BF16))}
            mtv = mtab_d.rearrange("p (a b) -> p a b", a=24, b=256)
            for di_ in (1, 2):
                for w_ in range(2):
                    memset("pool", qkp[di_][w_][64:128, :], 0.0, [("qkpz", di_, w_)])

            def load_btab(h):
                dma("pool", btab[h % 2], mtv[:, 3 * h:3 * h + 3, :], (), [("btab", h % 2)])

            def load_perm(h, di):
                src = qk4_s if di == 1 else qk16_s
                for w_ in range(2):
                    dma("sp", qkp[di][w_][0:64, :], src[w_][h * 64:(h + 1) * 64, :], (), [("qkp", di, w_)])
            for di in range(3):
                memset("pool", VD[di][:, :, 64:128], 1.0, [("VD1", di)])
            for b_ in range(2):
                memset("pool", qh[b_][64:128, :], 0.0, [("qz", b_)])
                memset("pool", kh[b_][64:128, :], 0.0, [("kz", b_)])

            def load_qk(h):
                hb = h % 2
                dma("sp", qh[hb][0:64, :], qd_s[h * 64:(h + 1) * 64, :], (), [("qh", hb)])
                dma("sp", kh[hb][0:64, :], kd_s[h * 64:(h + 1) * 64, :], (), [("kh", hb)])

            def load_v(h, di):
                dil = DILS[di]
                nb = 32 // dil
                src = vd_s.rearrange("(kb p r) c -> p r kb c", kb=nb, p=128, r=dil)
                dst = VD[di].rearrange("p (r kb) e -> p r kb e", r=dil, kb=nb)
                cols = slice(h * 64, (h + 1) * 64)
                if dil == 1:
                    for g in range(4):
                        dma("sp", dst[:, 0, g * 8:(g + 1) * 8, 0:64], src[:, 0, g * 8:(g + 1) * 8, cols], (), [("VD", di)])
                elif dil == 4:
                    for r in range(4):
                        dma("sp", dst[:, r, :, 0:64], src[:, r, :, cols], (), [("VD", di)])
                else:
                    for kb in range(2):
                        for g in range(2):
                            dma("sp", dst[:, g * 8:(g + 1) * 8, kb, 0:64], src[:, g * 8:(g + 1) * 8, kb, cols],
                                (), [("VD", di)])

            st_rr3 = [0]
            pt_rr3 = [0]
            ob_rr = [0]

            def dil_tiles(h, di):
                hb = h % 2
                dil = DILS[di]
                nb = 32 // dil
                gbank = {}
                gdone = {}

                def slot(r, n):
                    if dil == 16:
                        return r // 2, (r % 2) * 256 + n * 128
                    return r * (nb // 4) + n // 4, (n % 4) * 128

                def obank(gid):
                    if gid not in gbank:
                        gbank[gid] = 3 + ob_rr[0]
                        ob_rr[0] = (ob_rr[0] + 1) % 5
                        gdone[gid] = 0
                    return gbank[gid]

                def finish(gid):
                    ob = gbank[gid]
                    if dil == 1:
                        av = acc[hb][:, gid * 512:(gid + 1) * 512]
                        ov = banks[ob]
                    elif dil == 4:
                        r, g = gid // 2, gid % 2
                        av = acc[hb].rearrange("p (i r) -> p r i", r=4)[:, r, g * 512:(g + 1) * 512]
                        ov = banks[ob]
                    else:
                        av = acc[hb].rearrange("p (i r) -> p r i", r=16)[:, 2 * gid:2 * gid + 2, :]
                        ov = banks[ob].rearrange("p (a b) -> p a b", a=2, b=256)
                    tt("dve", av, ov, av, ALU.add, [("ps", ob), ("acc", hb)], [("acc", hb)], strict=True)

                def do_pv(item):
                    r, kb, pb, c0 = item
                    blk = r * nb + kb
                    rd = [("VD", di), ("VD1", di), ("PTd", pb)]
                    gid, col = slot(r, kb)
                    ob = obank(gid)
                    mm(banks[ob][:, col:col + 128], VD[di][:, blk, :], PTd[pb][:, c0:c0 + 128], kb == 0, True,
                       rd, [("ps", ob)])
                    if kb + 1 < nb:
                        gid2, col2 = slot(r, kb + 1)
                        ob2 = obank(gid2)
                        mm(banks[ob2][:, col2:col2 + 128], VD[di][:, blk, :], PTd[pb][:, c0 + 128:c0 + 256], True, False,
                           rd, [("ps", ob2)])
                    gdone[gid] += 1
                    if gdone[gid] == 4:
                        finish(gid)

                tiles = [(r, kb) for r in range(dil) for kb in range(nb)]
                pend = []
                for p0 in range(0, len(tiles), 2):
                    sb = st_rr3[0]
                    st_rr3[0] = (sb + 1) % 3
                    pb = pt_rr3[0]
                    pt_rr3[0] = (pb + 1) % 4
                    wend = 0
                    for t_i in range(2):
                        r, kb = tiles[p0 + t_i]
                        base = r + dil * 128 * kb
                        nq = 256 if kb + 1 < nb else 128
                        c0 = t_i * 256
                        wend = c0 + nq
                        if di == 0:
                            k_ap = kh[hb][:, base:base + 128]
                            q_ap = qh[hb][:, base:base + nq]
                            rd_ = [("kh", hb), ("qh", hb), ("qz", hb), ("kz", hb)]
                        else:
                            p0_ = r * (S // dil) + 128 * kb
                            k_ap = qkp[di][1][:, p0_:p0_ + 128]
                            q_ap = qkp[di][0][:, p0_:p0_ + nq]
                            rd_ = [("qkp", di, 0), ("qkp", di, 1), ("qkpz", di, 0), ("qkpz", di, 1)]
                        mm(banks[sb][:, c0:c0 + nq], k_ap, q_ap, True, False, rd_, [("ps", sb)])
                        mm(banks[sb][:, c0:c0 + nq], identb, btab[hb][:, di, 0:nq], False, True,
                           ["ident", ("btab", hb)], [("ps", sb)])
                        pend.append((r, kb, pb, c0))
                    act(PTd[pb][:, 0:wend], banks[sb][:, 0:wend], AF.Exp, [("ps", sb)], [("PTd", pb)], scale=0.125)
                    while len(pend) > 2:
                        do_pv(pend.pop(0))
                    n_pair[0] += 1
                    if norm_q and n_pair[0] % 2 == 0:
                        norm_q.pop(0)()
                while pend:
                    do_pv(pend.pop(0))

            norm_q = []
            n_pair = [0]

            def norm_ops(h, q4):
                hb = h % 2
                po = (h % 2) * 64
                cs = slice(q4 * 1024, (q4 + 1) * 1024)
                return [
                    lambda: cp("dve", rect[0:64, :], acc[hb][64:128, cs], [("acc", hb)], ["rect"]),
                    lambda: act(rect[0:64, :], rect[0:64, :], AF.Ln, ["rect"], ["rect"]),
                    lambda: act(rect[0:64, :], rect[0:64, :], AF.Exp, ["rect"], ["rect"], scale=-1.0),
                    lambda: tt("dve", attnT[po:po + 64, 4 + h // 2, cs], acc[hb][0:64, cs], rect[0:64, :], ALU.mult,
                               [("acc", hb), "rect"], [("attnD", h, q4)]),
                ]

            load_qk(0)
            load_btab(0)
            for di in range(3):
                load_v(0, di)
            load_perm(0, 1)
            load_perm(0, 2)
            for h in range(8):
                hb = h % 2
                memset("pool", acc[hb], 0.0, [("acc", hb)])
                if h + 1 < 8:
                    load_qk(h + 1)
                    load_btab(h + 1)
                for di in range(3):
                    dil_tiles(h, di)
                    if h + 1 < 8:
                        load_v(h + 1, di)
                        if di > 0:
                            load_perm(h + 1, di)
                for q4 in range(4):
                    norm_q.extend(norm_ops(h, q4))
            while norm_q:
                norm_q.pop(0)()
            dump("attn_all", attnT, [128, 8, S], BF16, [("attnD", h, q) for h in range(8) for q in range(4)])
            sch.flush()

        x1T = carve(A0, [8, S], BF16)

        def layer_norm(xr_ap, key, g_ap, b_ap, sm, smk):
            st6, mv, lnv, rs, nmr = sm
            for hf in range(2):
                sch.add("dve", lambda e, hf=hf: e.bn_stats(out=st6[:, hf, :], in_=xr_ap[:, hf * 512:(hf + 1) * 512]),
                        [key], [(smk, "st", hf)], strict=True)
            sch.add("dve", lambda e: e.bn_aggr(out=mv, in_=st6.rearrange("p a b -> p (a b)")),
                    [(smk, "st", 0), (smk, "st", 1)], [(smk, "mv")], strict=True)
            act(lnv, mv[:, 1:2], AF.Ln, [(smk, "mv"), "epsl"], [(smk, "lnv")], scale=1.0, bias=epsl, strict=True)
            act(rs, lnv, AF.Exp, [(smk, "lnv")], [(smk, "rs")], scale=-0.5, strict=True)
            ts("dve", nmr, mv[:, 0:1], rs, -1.0, ALU.mult, ALU.mult, [(smk, "mv"), (smk, "rs")], [(smk, "nmr")],
               strict=True)
            act(xr_ap, xr_ap, AF.Identity, [key, (smk, "rs"), (smk, "nmr")], [key], scale=rs, bias=nmr)
            tt("dve", xr_ap, xr_ap, g_ap, ALU.mult, [key, "lnp"], [key])
            tt("pool", xr_ap, xr_ap, b_ap, ALU.add, [key, "lnp"], [key])

        if stop_after >= 4:
            c = Bump(C0, CTOP)
            wob = c([8, 1024], BF16)
            lnp1 = c([2, 1024], F32)
            NXR = 8
            xr3 = [c([1024], F32) for _ in range(NXR)]
            yb = [c([1024], BF16) for _ in range(2)]
            NSM = 4
            sms = [(c([2, 6], F32), c([2], F32), c([1], F32), c([1], F32), c([1], F32)) for _ in range(NSM)]
            wov = wo_d.rearrange("(k p) n -> p k n", p=128)
            for k in range(0, 8, 4):
                dma("pool", wob[:, k:k + 4, :], wov[:, k:k + 4, :], (), [("wob", k // 4)])
            dma("sp", lnp1, lnp_d[0:2].rearrange("a p n -> p a n"), (), ["lnp"])

            def ln_stats(xr, key, sm, smk):
                st6, mv, lnv, rs, nmr = sm
                for hf in range(2):
                    sch.add("dve", lambda e, hf=hf: e.bn_stats(out=st6[:, hf, :], in_=xr[:, hf * 512:(hf + 1) * 512]),
                            [key], [(smk, "st", hf)], strict=True)
                sch.add("dve", lambda e: e.bn_aggr(out=mv, in_=st6.rearrange("p a b -> p (a b)")),
                        [(smk, "st", 0), (smk, "st", 1)], [(smk, "mv")], strict=True)

            def ln_rstd(sm, smk):
                st6, mv, lnv, rs, nmr = sm
                act(lnv, mv[:, 1:2], AF.Ln, [(smk, "mv"), "epsl"], [(smk, "lnv")], scale=1.0, bias=epsl, strict=True)
                act(rs, lnv, AF.Exp, [(smk, "lnv")], [(smk, "rs")], scale=-0.5, strict=True)
                ts("dve", nmr, mv[:, 0:1], rs, -1.0, ALU.mult, ALU.mult, [(smk, "mv"), (smk, "rs")], [(smk, "nmr")],
                   strict=True)

            def ln_norm(xr, key, sm, smk):
                st6, mv, lnv, rs, nmr = sm
                act(xr, xr, AF.Identity, [key, (smk, "rs"), (smk, "nmr")], [key], scale=rs, bias=nmr)

            def ln_affine(xr, key, g_ap, b_ap):
                tt("dve", xr, xr, g_ap, ALU.mult, [key, "lnp"], [key])
                tt("pool", xr, xr, b_ap, ALU.add, [key, "lnp"], [key])

            def s0(i):
                tk = slice(i * 128, (i + 1) * 128)
                pb = (i % 3) * 2
                dma("sp", xr3[i % NXR], x_d[tk, :], (), [("xr", i % NXR)])
                for hf in range(2):
                    for cc in range(8):
                        mm(banks[pb + hf], attnT[:, cc, tk], wob[:, cc, hf * 512:(hf + 1) * 512],
                           cc == 0, cc == 7, ["attnT", ("wob", 0), ("wob", 1)], [("ps", pb + hf)])

            def s1(i):
                pb = (i % 3) * 2
                key = ("xr", i % NXR)
                xr = xr3[i % NXR]
                for hf in range(2):
                    hs = slice(hf * 512, (hf + 1) * 512)
                    stt("dve", xr[:, hs], xr[:, hs], DN_ALPHA, banks[pb + hf], ALU.mult, ALU.add,
                        [key, ("ps", pb + hf)], [key])
                ln_stats(xr, key, sms[i % NSM], ("sm", i % NSM))

            def s2(i):
                ln_rstd(sms[i % NSM], ("sm", i % NSM))

            def s3(i):
                ln_norm(xr3[i % NXR], ("xr", i % NXR), sms[i % NSM], ("sm", i % NSM))

            def s4(i):
                ln_affine(xr3[i % NXR], ("xr", i % NXR), lnp1[:, 0, :], lnp1[:, 1, :])

            def s5(i):
                tk = slice(i * 128, (i + 1) * 128)
                key = ("xr", i % NXR)
                dma("sp", x1_s[tk, :], xr3[i % NXR], [key], [("x1s", i)])
                act(yb[i % 2], xr3[i % NXR], AF.Copy, [key], [("yb", i % 2)])

            def s6(i):
                tk = slice(i * 128, (i + 1) * 128)
                tb = 6 + (i % 2)
                pT = banks[tb].bitcast(BF16).rearrange("p (a b) -> p a b", a=8, b=128)
                for cc in range(8):
                    tr(pT[:, cc, :], yb[i % 2][:, cc * 128:(cc + 1) * 128], identb, [("yb", i % 2), "ident"], [("ps", tb)])
                cp("act", x1T[:, :, tk], pT, [("ps", tb)], [("x1T", i)])

            stages = [(s6, 7), (s5, 6), (s4, 5), (s3, 4), (s2, 3), (s1, 2), (s0, 0)]
            for t in range(32 + 7):
                for fn_, lag in stages:
                    i = t - lag
                    if 0 <= i < 32:
                        fn_(i)
            dump("x1T", x1T, [128, 8, S], BF16, [("x1T", i) for i in range(32)])
            sch.flush()

        if stop_after >= 5:
            bb = Bump(B0, C0)
            c = Bump(C0, CTOP)
            hmid = bb([NJ, 1024], BF16)
            wupb = [bb([8, 256], BF16) for _ in range(3)]
            yy = [[bb([1024], F32), c([1024], F32)] for _ in range(2)]
            sms = [(c([2, 6], F32), c([2], F32), c([1], F32), c([1], F32), c([1], F32)) for _ in range(2)]
            wdb = c([NJ, 1024], BF16)
            lnp2 = c([2, 1024], F32)
            cw = c([44, 4], F32)
            halo = c([44, 2], F32)
            xr2 = [c([1024], F32) for _ in range(2)]
            dma("sp", lnp2, lnp_d[2:4].rearrange("a p n -> p a n"), (), ["lnp"])
            dma("sp", cw, cw_d.rearrange("p (a b) -> p a b", a=44, b=4), (), ["cw"])
            memset("pool", halo, 0.0, ["halo"])
            wdv = wd_d.rearrange("(j p) n -> p j n", p=128)
            wupv = wup_d.rearrange("j (k p) n -> j p k n", p=128)

            def load_wd(piece):
                j0 = 2 * piece
                dma("pool", wdb[:, j0:j0 + 2, :], wdv[:, j0:j0 + 2, :], (), [("wdb", piece)])

            seq = [(s, j) for s in range(4) for j in range(NJ)]

            def load_wup(it):
                dma("pool", wupb[it % 3], wupv[seq[it][1]], (), [("wupb", it % 3)])

            def up_mm(it):
                s, j = seq[it]
                q = it % 2
                for hf in range(2):
                    for t2 in range(2):
                        b = 4 * q + 2 * hf + t2
                        tok0 = s * 1024 + t2 * 512
                        for k in range(8):
                            mm(banks[b], wupb[it % 3][:, k, hf * 128:(hf + 1) * 128], x1T[:, k, tok0:tok0 + 512],
                               k == 0, k == 7, [("wupb", it % 3), "x1T"], [("ps", b)])

            def up_conv(it):
                s, j = seq[it]
                q = it % 2
                for hf in range(2):
                    ch = j + NJ * hf
                    b0 = 4 * q + 2 * hf
                    U = psum[:, b0 * 512:(b0 + 2) * 512]
                    ur = [("ps", b0), ("ps", b0 + 1)]
                    y = yy[hf][q]
                    yk = ("yy", hf, q)
                    hk = ("halo", ch)
                    act(y, U, AF.Identity, ur + ["cw"], [yk], scale=cw[:, ch, 2:3], bias=cw[:, ch, 3:4])
                    stt("dve", y[:, 1:1024], U[:, 0:1023], cw[:, ch, 1:2], y[:, 1:1024], ALU.mult, ALU.add,
                        ur + ["cw", yk], [yk])
                    stt("dve", y[:, 2:1024], U[:, 0:1022], cw[:, ch, 0:1], y[:, 2:1024], ALU.mult, ALU.add,
                        ur + ["cw", yk], [yk])
                    stt("dve", y[:, 0:2], halo[:, ch, 0:2], cw[:, ch, 0:1], y[:, 0:2], ALU.mult, ALU.add,
                        [hk, "halo", "cw", yk], [yk], strict=True)
                    stt("dve", y[:, 0:1], halo[:, ch, 1:2], cw[:, ch, 1:2], y[:, 0:1], ALU.mult, ALU.add,
                        [hk, "halo", "cw", yk], [yk], strict=True)
                    cp("dve", halo[:, ch, :], U[:, 1022:1024], ur, [hk], strict=True)

            def up_gate(it):
                s, j = seq[it]
                q = it % 2
                act(yy[1][q], yy[1][q], AF.Gelu_apprx_tanh, [("yy", 1, q)], [("yy", 1, q)])
                tt("pool", hmid[:, j, :], yy[1][q], yy[0][q], ALU.mult, [("yy", 1, q), ("yy", 0, q)], [("hmid", j)])

            def load_x1(n):
                tk0 = (n // 8) * 1024 + (n % 8) * 128
                dma("sp", xr2[n % 2], x1_s[tk0:tk0 + 128, :], [("x1s", n)], [("xr", n % 2)])

            n_dn = [0]

            def down(s, sub):
                tk0 = s * 1024 + sub * 128
                tk = slice(tk0, tk0 + 128)
                n = s * 8 + sub
                xb2 = n % 2
                key = ("xr", xb2)
                pb = (n_dn[0] % 4) * 2
                n_dn[0] += 1
                for hf in range(2):
                    for j in range(NJ):
                        mm(banks[pb + hf], hmid[:, j, sub * 128:(sub + 1) * 128], wdb[:, j, hf * 512:(hf + 1) * 512],
                           j == 0, j == NJ - 1, [("hmid", j), ("wdb", j // 2)], [("ps", pb + hf)])
                if n + 1 < 32:
                    load_x1(n + 1)
                for hf in range(2):
                    hs = slice(hf * 512, (hf + 1) * 512)
                    stt("dve", xr2[xb2][:, hs], xr2[xb2][:, hs], DN_ALPHA, banks[pb + hf], ALU.mult, ALU.add,
                        [key, ("ps", pb + hf)], [key])
                layer_norm(xr2[xb2], key, lnp2[:, 0, :], lnp2[:, 1, :], sms[n % 2], ("sm", n % 2))
                dma("sp", out_d[tk, :], xr2[xb2], [key], [("out", n)])

            load_wup(0)
            load_wup(1)
            load_x1(0)
            for it in range(len(seq)):
                s, j = seq[it]
                if it + 2 < len(seq):
                    load_wup(it + 2)
                if 1 <= it <= NJ // 2:
                    load_wd(it - 1)
                up_mm(it)
                up_conv(it)
                if j > 0:
                    up_gate(it - 1)
                if j == NJ - 1:
                    up_gate(it)
                    for sub in range(8):
                        down(s, sub)
            sch.flush()
        else:
            sch.flush()
    build.last_log = sch.log
    return nc, dump_out


def _tables():
    f32 = np.float32
    half = 16
    freqs = np.power(f32(10000.0), -(np.arange(half, dtype=f32) / f32(half))).astype(f32)
    ang = (np.arange(S, dtype=f32)[:, None] * freqs[None, :]).astype(f32)
    cos = np.cos(ang).astype(f32).T
    sin = np.sin(ang).astype(f32).T
    cc = np.concatenate([cos, cos], 0)
    ss = np.concatenate([-sin, sin], 0)
    qtab = np.concatenate([np.ones((64, S), f32), cc, ss], 0)
    z = np.zeros((64, S), f32)
    ktab = np.stack([np.concatenate([z, cc, cc], 0), np.concatenate([z, ss, ss], 0)], 0)
    kk = np.arange(128)[:, None]
    qq = np.arange(128)[None, :]
    tri = np.where(kk <= qq, 0.0, -30000.0).astype(f32)
    slopes = 2.0 ** (-8.0 * np.arange(1, 9, dtype=np.float64) / 8.0)
    ql = np.arange(256)[None, :]
    off = ql - kk
    valid = (off >= 0) & (off <= 128)
    mt = np.zeros((128, 24, 256), f32)
    for h in range(8):
        for di, dil in enumerate(DILS):
            bias = -8.0 * slopes[h] * dil * off.astype(np.float64)
            mt[:, h * 3 + di, :] = np.where(valid, bias, -240000.0).astype(f32)
    return qtab, ktab, tri, mt.reshape(128, 24 * 256)


def _prep_shared(inp):
    f32 = np.float32
    w_in = np.asarray(inp["w_in"], f32)
    kr = w_in[:, 384:416]
    krs = np.concatenate([kr[:, 16:32], kr[:, 0:16]], 1)
    w1 = np.concatenate([w_in[:, 0:384], kr, kr, kr, kr, krs, krs, krs, krs, w_in[:, 416:1952]], 1)
    assert w1.shape == (D, 2176)
    w_uq = np.asarray(inp["w_uq"], f32)
    rope = w_uq[:, :, 64:96]
    swap = np.concatenate([rope[:, :, 16:32], rope[:, :, 0:16]], 2)
    wq = np.concatenate([w_uq, swap], 2).reshape(256, 1024)
    gq = np.zeros((128, 4), f32)
    gq[:, 0:2] = np.asarray(inp["g_cq"], f32).reshape(2, 128).T
    gq[:, 2] = np.asarray(inp["g_ckv"], f32)
    qtab, ktab, tri, mtab = _tables()
    w_up = np.asarray(inp["w_up"], f32)
    wup = np.stack([np.concatenate([w_up[:, j * 128:(j + 1) * 128], w_up[:, DFF + j * 128:DFF + (j + 1) * 128]], 1)
                    for j in range(NJ)], 0)
    conv_w = np.asarray(inp["conv_w"], f32)
    conv_b = np.asarray(inp["conv_b"], f32)
    cw = np.concatenate([conv_w, conv_b[None, :]], 0)
    cw = cw.reshape(4, 44, 128).transpose(2, 1, 0).reshape(128, 44 * 4)
    lnp = np.stack([np.broadcast_to(np.asarray(inp[k], f32)[None, :], (128, D))
                    for k in ("ln1_g", "ln1_b", "ln2_g", "ln2_b")], 0)
    c = np.ascontiguousarray
    return {
        "w1": c(w1), "wq": c(wq), "wuk": c(np.asarray(inp["w_uk"], f32).reshape(128, 512)),
        "wuv": c(np.asarray(inp["w_uv"], f32).reshape(128, 512)), "gq": gq, "qtab": c(qtab), "ktab": c(ktab),
        "tri": c(tri), "mtab": c(mtab), "wo": c(np.asarray(inp["w_o"], f32)), "wup": c(wup), "cw": c(cw),
        "wd": c(np.asarray(inp["w_down"], f32)), "lnp": c(lnp), "ident": np.eye(128, dtype=f32),
    }


def kernel(**inputs):
    x = np.asarray(inputs["x"], np.float32)
    B = x.shape[0]
    shared = _prep_shared(inputs)
    nc, _ = build()
    in_maps = []
    for b in range(B):
        m = dict(shared)
        m["x"] = np.ascontiguousarray(x[b])
        m["xT"] = np.ascontiguousarray(x[b].T)
        in_maps.append(m)
    res = run_bass_kernel_spmd(nc, in_maps, core_ids=list(range(B)))
    return np.stack([np.asarray(r["out"], np.float32) for r in res.results], 0)
```

```python
import math
from contextlib import ExitStack

import numpy as np
import concourse.bass as bass
import concourse.mybir as mybir
from concourse.bass_utils import run_bass_kernel_spmd

F32 = mybir.dt.float32
BF16 = mybir.dt.bfloat16
ALU = mybir.AluOpType
AF = mybir.ActivationFunctionType

S = 4096
D = 1024
DFF = 2816
NJ = 22
DN_ALPHA = 2.0 ** 0.25
LN_EPS = 1e-5
RMS_EPS = 1e-6
DILS = (1, 4, 16)
KB = 1024

SAME_ENG_SYNC = True
ENGS = ("pe", "act", "dve", "pool", "sp")


class _Op:
    __slots__ = ("eng", "fn", "deps", "dma", "signal", "sem", "val", "prev", "strict")


class Sched:
    def __init__(self, nc, stack, n_dma=36):
        self.nc = nc
        self.ops = []
        self.flushed = 0
        self.last_w = {}
        self.rd_eng = {}
        self.rd_dma = {}
        self.sems = {}
        for e in ENGS:
            self.sems[("e", e)] = stack.enter_context(nc.semaphore("s_" + e))
        for i in range(n_dma):
            self.sems[("d", i)] = stack.enter_context(nc.semaphore("d%d" % i))
        self.n_dma = n_dma
        self.eng_cnt = {e: 0 for e in ENGS}
        self.dma_val = [0] * n_dma
        self.dma_rr = 0
        self.sw_rr = 0
        self.n_hw = 20
        self.known = {e: {} for e in ENGS}
        self.log = {}

    def add(self, eng, fn, reads=(), writes=(), dma=False, strict=False):
        i = len(self.ops)
        if eng != "pe":
            writes = list(writes) + [r for r in reads if isinstance(r, tuple) and r[0] == "ps" and r not in writes]
        deps = set()
        for r in reads:
            if r in self.last_w:
                deps.add(self.last_w[r])
        for w in writes:
            if w in self.last_w:
                deps.add(self.last_w[w])
            deps.update(self.rd_eng.get(w, {}).values())
            deps.update(self.rd_dma.get(w, ()))
        for r in reads:
            if dma:
                self.rd_dma.setdefault(r, []).append(i)
            else:
                self.rd_eng.setdefault(r, {})[eng] = i
        for w in writes:
            self.last_w[w] = i
            self.rd_eng[w] = {}
            self.rd_dma[w] = []
        op = _Op()
        op.eng, op.fn, op.deps, op.dma = eng, fn, deps, dma
        op.signal, op.sem, op.val, op.prev = False, None, 0, 0
        op.strict = strict
        self.ops.append(op)
        return i

    def _skip(self, p, op):
        if p.dma or op.dma:
            return False
        if p.eng != op.eng:
            return False
        if p.eng == "pe":
            return True
        return not (SAME_ENG_SYNC or p.strict or op.strict)

    def flush(self):
        nc = self.nc
        ops = self.ops[self.flushed:]
        self.flushed = len(self.ops)
        for op in ops:
            for d in op.deps:
                p = self.ops[d]
                if not p.dma and not self._skip(p, op):
                    p.signal = True
        last = {}
        for op in ops:
            if not op.dma:
                last[op.eng] = op
        for op in last.values():
            op.signal = True
        for op in ops:
            if op.dma:
                if op.eng == "pool":
                    k = self.n_hw + self.sw_rr
                    self.sw_rr = (self.sw_rr + 1) % (self.n_dma - self.n_hw)
                else:
                    k = self.dma_rr
                    self.dma_rr = (k + 1) % self.n_hw
                op.sem = ("d", k)
                op.prev = self.dma_val[k]
                self.dma_val[k] += 16
                op.val = self.dma_val[k]
            elif op.signal:
                self.eng_cnt[op.eng] += 1
                op.sem = ("e", op.eng)
                op.val = self.eng_cnt[op.eng]

        def emit(name, e):
            known = self.known[name]

            def wait(key, val):
                if val <= 0 or known.get(key, 0) >= val:
                    return
                e.wait_ge(self.sems[key], val)
                known[key] = val
                self.log.setdefault(name, []).append(("w", key, val))

            for op in ops:
                if op.eng != name:
                    continue
                need = {}
                for d in op.deps:
                    p = self.ops[d]
                    if p.sem is None or self._skip(p, op):
                        continue
                    need[p.sem] = max(need.get(p.sem, 0), p.val)
                if op.dma and op.prev > 0:
                    need[op.sem] = max(need.get(op.sem, 0), op.prev)
                for k, v in need.items():
                    wait(k, v)
                ins = op.fn(e)
                if op.dma:
                    ins.then_inc(self.sems[op.sem], 16)
                    self.log.setdefault(name, []).append(("i", op.sem, 16))
                elif op.signal:
                    ins.then_inc(self.sems[op.sem], 1)
                    self.log.setdefault(name, []).append(("i", op.sem, 1))
            for en in ENGS:
                if en != name:
                    wait(("e", en), self.eng_cnt[en])
            for k in range(self.n_dma):
                wait(("d", k), self.dma_val[k])

        with nc.Block() as block:
            @block.tensor
            def _(e):
                emit("pe", e)

            @block.scalar
            def _(e):
                emit("act", e)

            @block.vector
            def _(e):
                emit("dve", e)

            @block.gpsimd
            def _(e):
                emit("pool", e)

            @block.sync
            def _(e):
                emit("sp", e)

        self.last_w.clear()
        self.rd_eng.clear()
        self.rd_dma.clear()


def _dsize(dtype):
    return 4 if dtype == F32 else 2


def build(stop_after=5, dumps=()):
    nc = bass.Bass("TRN2", target_bir_lowering=False)

    def din(name, shape, dtype=F32):
        return nc.dram_tensor(name, shape, dtype, kind="ExternalInput").ap()

    xT_d = din("xT", [D, S])
    x_d = din("x", [S, D])
    w1_d = din("w1", [D, 2176])
    wq_d = din("wq", [256, 1024])
    wuk_d = din("wuk", [128, 512])
    wuv_d = din("wuv", [128, 512])
    gq_d = din("gq", [128, 4])
    qtab_d = din("qtab", [128, S])
    ktab_d = din("ktab", [2, 128, S])
    tri_d = din("tri", [128, 128])
    mtab_d = din("mtab", [128, 24 * 256])
    wo_d = din("wo", [D, D])
    wup_d = din("wup", [NJ, D, 256])
    cw_d = din("cw", [128, 44 * 4])
    wd_d = din("wd", [DFF, D])
    lnp_d = din("lnp", [4, 128, D])
    ident_d = din("ident", [128, 128])
    out_d = nc.dram_tensor("out", [S, D], F32, kind="ExternalOutput").ap()
    qd_s = nc.dram_tensor("qd_s", [512, S], BF16, kind="Internal").ap()
    kd_s = nc.dram_tensor("kd_s", [512, S], BF16, kind="Internal").ap()
    qk4_s = [nc.dram_tensor(n_, [512, S], BF16, kind="Internal").ap() for n_ in ("qd4_s", "kd4_s")]
    qk16_s = [nc.dram_tensor(n_, [512, S], BF16, kind="Internal").ap() for n_ in ("qd16_s", "kd16_s")]
    vd_s = nc.dram_tensor("vd_s", [S, 512], BF16, kind="Internal").ap()
    x1_s = nc.dram_tensor("x1_s", [S, D], F32, kind="Internal").ap()

    dump_out = {}

    with ExitStack() as stack:
        ARENA_BYTES = 206 * KB
        arena = stack.enter_context(nc.sbuf_tensor("arena", [128, ARENA_BYTES // 2], BF16))
        psum = stack.enter_context(nc.psum_tensor("psum", [128, 4096], F32))
        banks = [psum[:, i * 512:(i + 1) * 512] for i in range(8)]
        sch = Sched(nc, stack)

        def carve(off, shape, dtype):
            n = 1
            for v in shape:
                n *= v
            nb = n * _dsize(dtype)
            assert off % 4 == 0 and off + nb <= ARENA_BYTES, (off, nb)
            ap = arena[:, off // 2:(off + nb) // 2]
            if dtype != BF16:
                ap = ap.bitcast(dtype)
            if len(shape) == 2:
                ap = ap.rearrange("p (a b) -> p a b", a=shape[0], b=shape[1])
            elif len(shape) == 3:
                ap = ap.rearrange("p (a b c) -> p a b c", a=shape[0], b=shape[1], c=shape[2])
            return ap

        class Bump:
            def __init__(self, start, end):
                self.off, self.end = start, end

            def __call__(self, shape, dtype):
                n = 1
                for v in shape:
                    n *= v
                nb = (n * _dsize(dtype) + 63) // 64 * 64
                ap = carve(self.off, shape, dtype)
                self.off += nb
                assert self.off <= self.end, (self.off, self.end)
                return ap

        A0, B0, C0, CEND = 0, 64 * KB, 128 * KB, ARENA_BYTES

        def mm(out, lhsT, rhs, start, stop, reads, writes):
            sch.add("pe", lambda e: e.matmul(out, lhsT=lhsT, rhs=rhs, start=start, stop=stop), reads, writes)

        def tr(out, in_, ident, reads, writes):
            sch.add("pe", lambda e: e.transpose(out, in_, ident), reads, writes)

        def act(out, in_, func, reads, writes, scale=None, bias=None, strict=False):
            kw = {}
            if scale is not None:
                kw["scale"] = scale
            if bias is not None:
                kw["bias"] = bias
            sch.add("act", lambda e: e.activation(out=out, in_=in_, func=func, **kw), reads, writes, strict=strict)

        def tt(eng, out, in0, in1, op, reads, writes, strict=False):
            sch.add(eng, lambda e: e.tensor_tensor(out=out, in0=in0, in1=in1, op=op), reads, writes, strict=strict)

        def stt(eng, out, in0, scalar, in1, op0, op1, reads, writes, strict=False):
            sch.add(eng, lambda e: e.scalar_tensor_tensor(out=out, in0=in0, scalar=scalar, in1=in1,
                                                          op0=op0, op1=op1), reads, writes, strict=strict)

        def ts(eng, out, in0, s1, s2, op0, op1, reads, writes, strict=False):
            sch.add(eng, lambda e: e.tensor_scalar(out=out, in0=in0, scalar1=s1, scalar2=s2, op0=op0, op1=op1),
                    reads, writes, strict=strict)

        def cp(eng, out, in_, reads, writes, strict=False):
            if eng == "act":
                act(out, in_, AF.Copy, reads, writes, strict=strict)
            else:
                sch.add(eng, lambda e: e.tensor_copy(out=out, in_=in_), reads, writes, strict=strict)

        def recip(out, in_, reads, writes):
            sch.add("dve", lambda e: e.reciprocal(out=out, in_=in_), reads, writes, strict=True)

        def memset(eng, ap, val, writes):
            sch.add(eng, lambda e: e.memset(ap, val), (), writes)

        def dma(eng, out, in_, reads, writes):
            sch.add(eng, lambda e: e.dma_start(out=out, in_=in_), reads, writes, dma=True)

        def dump(name, ap, shape, dtype, reads):
            if name not in dumps:
                return
            o = nc.dram_tensor("dbg_" + name, shape, dtype, kind="ExternalOutput").ap()
            dump_out[name] = o
            dma("sp", o, ap, reads, [("dbg", name)])

        evac_rr = [0]

        def evac(out, in_, reads, writes):
            evac_rr[0] ^= 1
            cp("act" if evac_rr[0] else "dve", out, in_, reads, writes)

        cst = Bump(CEND - 2 * KB, CEND)
        epsr = cst([1], F32)
        epsl = cst([1], F32)
        onesf = cst([128], F32)
        identb = cst([128], BF16)
        memset("pool", epsr, RMS_EPS, ["epsr"])
        memset("pool", epsl, LN_EPS, ["epsl"])
        memset("pool", onesf, 1.0, ["onesf"])
        dma("pool", identb, ident_d, (), ["ident"])
        CTOP = CEND - 2 * KB

        cq = carve(C0, [2, S], BF16)
        ckv = carve(C0 + 16 * KB, [S], BF16)
        krd = carve(C0 + 24 * KB, [S], BF16)

        a = Bump(A0, B0)
        c = Bump(C0 + 32 * KB, CTOP)
        w1b = a([8, 2176], BF16)
        xt = [a([8, 512], BF16) for _ in range(2)]
        stg = [a([512], BF16) for _ in range(4)]
        stg4 = [a([512], BF16) for _ in range(4)]
        P16 = carve(B0, [8, S], BF16)
        s4_rr = [0]
        gq = c([4], F32)
        raw = c([3, 512], F32)
        sq = c([3, 512], F32)
        rstd = c([2, 512], F32)
        lnt = c([2, 512], F32)
        ktab = [c([2, 512], F32) for _ in range(2)]
        t1 = c([512], F32)
        t2 = c([512], F32)

        w1v = w1_d.rearrange("(k p) n -> p k n", p=128)
        xTv = xT_d.rearrange("(k p) t -> p k t", p=128)
        for k in range(0, 8, 4):
            dma("pool", xt[0][:, k:k + 4, :], xTv[:, k:k + 4, 0:512], (), [("xt", 0, k // 4)])
        for k in range(8):
            dma("pool", w1b[:, k, 640:1664], w1v[:, k, 640:1664], (), [("w1c", k)])
        for k in range(8):
            dma("pool", w1b[:, k, 0:640], w1v[:, k, 0:640], (), [("w1b", k)])
        for k in range(8):
            dma("pool", w1b[:, k, 1664:2176], w1v[:, k, 1664:2176], (), [("w1v", k)])
        dma("sp", gq, gq_d, (), ["gq"])
        ktv = ktab_d.rearrange("a p t -> p a t")
        w1r = [("w1b", k) for k in range(8)]
        bank_rr = [0]
        stg_rr = [0]

        def next_bank(n=6):
            b = bank_rr[0]
            bank_rr[0] = (b + 1) % n
            return b

        def store_p16(hf_):
            for g in range(8):
                r0 = (g % 4) * 128
                dv = qk16_s[g // 4][r0:r0 + 128, :].rearrange("p (r j) -> p r j", r=16)
                sv = P16[:, g, :].rearrange("p (r j) -> p r j", r=16)
                for rh in range(2):
                    dma("sp", dv[:, rh * 8:(rh + 1) * 8, hf_ * 128:(hf_ + 1) * 128],
                        sv[:, rh * 8:(rh + 1) * 8, hf_ * 128:(hf_ + 1) * 128],
                        [("P16", g, tt_) for tt_ in range(hf_ * 4, hf_ * 4 + 4)], [("qkd16", g, hf_, rh)])

        for t in range(8):
            xb = t % 2
            tok = slice(t * 512, (t + 1) * 512)
            if t > 0:
                for k in range(0, 8, 4):
                    dma("pool", xt[xb][:, k:k + 4, :], xTv[:, k:k + 4, tok], (), [("xt", xb, k // 4)])
            dma("sp", ktab[xb][64:128], ktv[64:128, :, tok], (), [("ktab", xb)])
            xr = [("xt", xb, 0), ("xt", xb, 1)]
            for gi in list(range(5, 13)) + list(range(5)):
                if gi == 0 and t % 4 == 3:
                    store_p16(t // 4)
                b = next_bank()
                ps = banks[b]
                for k in range(8):
                    mm(ps[:, :], w1b[:, k, gi * 128:(gi + 1) * 128], xt[xb][:, k, :], k == 0, k == 7,
                       xr + [("w1b" if gi < 5 else "w1c", k)], [("ps", b)])
                if gi < 3:
                    act(raw[:, gi, :], ps[:, :], AF.Copy, [("ps", b)], [("raw", gi)])
                    act(sq[:, gi, :], ps[:, :], AF.Square, [("ps", b)], [("sq", gi)])
                    if gi == 1:
                        mm(banks[6][:, :], onesf, sq[:, 0, :], True, False, ["onesf", ("sq", 0)], [("ps", 6)])
                        mm(banks[6][:, :], onesf, sq[:, 1, :], False, True, ["onesf", ("sq", 1)], [("ps", 6)])
                        act(lnt[:, 0, :], banks[6][:, :], AF.Ln, [("ps", 6), "epsr"], [("lnt", 0)],
                            scale=1.0 / 256, bias=epsr)
                        act(rstd[:, 0, :], lnt[:, 0, :], AF.Exp, [("lnt", 0)], [("rstd", 0)], scale=-0.5)
                        for cc in range(2):
                            stt("dve", cq[:, cc, tok], raw[:, cc, :], gq[:, cc:cc + 1], rstd[:, 0, :],
                                ALU.mult, ALU.mult, [("raw", cc), "gq", ("rstd", 0)], [("cq", t)])
                    if gi == 2:
                        mm(banks[7][:, :], onesf, sq[:, 2, :], True, True, ["onesf", ("sq", 2)], [("ps", 7)])
                        act(lnt[:, 1, :], banks[7][:, :], AF.Ln, [("ps", 7), "epsr"], [("lnt", 1)],
                            scale=1.0 / 128, bias=epsr)
                        act(rstd[:, 1, :], lnt[:, 1, :], AF.Exp, [("lnt", 1)], [("rstd", 1)], scale=-0.5)
                        stt("dve", ckv[:, tok], raw[:, 2, :], gq[:, 2:3], rstd[:, 1, :],
                            ALU.mult, ALU.mult, [("raw", 2), "gq", ("rstd", 1)], [("ckv", t)])
                elif gi == 3:
                    tt("dve", t1[64:128, :], ps[64:128, :], ktab[xb][64:128, 0, :], ALU.mult,
                       [("ps", b), ("ktab", xb)], ["t1"])
                elif gi == 4:
                    tt("dve", t2[64:128, :], ps[64:128, :], ktab[xb][64:128, 1, :], ALU.mult,
                       [("ps", b), ("ktab", xb)], ["t2"])
                    tt("dve", krd[64:128, tok], t1[64:128, :], t2[64:128, :], ALU.add, ["t1", "t2"], [("krd", t)])
                else:
                    si = stg_rr[0]
                    stg_rr[0] = (si + 1) % 4
                    g = gi - 5
                    eng_ = "act" if g % 2 == 0 else "dve"
                    cp(eng_, stg[si], ps[:, :], [("ps", b)], [("stg", si)])
                    dst = qd_s if gi < 9 else kd_s
                    r0 = (g % 4) * 128
                    dma("sp", dst[r0:r0 + 128, tok], stg[si], [("stg", si)], [("qkd", gi, t)])
                    s4 = s4_rr[0]
                    s4_rr[0] = (s4 + 1) % 4
                    cp(eng_, stg4[s4].rearrange("p (r j) -> p r j", r=4), ps.rearrange("p (j r) -> p r j", r=4),
                       [("ps", b)], [("stg4", s4)])
                    d4 = qk4_s[0 if gi < 9 else 1][r0:r0 + 128, :].rearrange("p (r j) -> p r j", r=4)
                    dma("sp", d4[:, :, t * 128:(t + 1) * 128], stg4[s4].rearrange("p (r j) -> p r j", r=4),
                        [("stg4", s4)], [("qkd4", gi, t)])
                    cp(eng_, P16[:, g, :].rearrange("p (r j) -> p r j", r=16)[:, :, t * 32:(t + 1) * 32],
                       ps.rearrange("p (j r) -> p r j", r=16), [("ps", b)], [("P16", g, t)])
            for sub in range(4):
                b = next_bank()
                ps = banks[b]
                for k in range(8):
                    mm(ps[:, :], xt[xb][:, k, sub * 128:(sub + 1) * 128], w1b[:, k, 1664:2176], k == 0, k == 7,
                       xr + [("w1v", k)], [("ps", b)])
                si = stg_rr[0]
                stg_rr[0] = (si + 1) % 4
                evac(stg[si], ps[:, :], [("ps", b)], [("stg", si)])
                r0 = t * 512 + sub * 128
                dma("sp", vd_s[r0:r0 + 128, :], stg[si], [("stg", si)], [("vd", t, sub)])
        dump("cq", cq, [128, 2, S], BF16, [("cq", t) for t in range(8)])
        dump("ckv", ckv, [128, S], BF16, [("ckv", t) for t in range(8)])
        dump("krd", krd[64:128], [64, S], BF16, [("krd", t) for t in range(8)])
        sch.flush()
        if "qd_s" in dumps:
            for nm, src in (("qd_s", qd_s), ("kd_s", kd_s)):
                o = nc.dram_tensor("dbg_" + nm, [512, S], BF16, kind="ExternalOutput").ap()
                dump_out[nm] = o
                dma("sp", o, src, (), [("dbg", nm)])
            o = nc.dram_tensor("dbg_vd_s", [S, 512], BF16, kind="ExternalOutput").ap()
            dump_out["vd_s"] = o
            dma("sp", o, vd_s, (), [("dbg", "vd_s")])
            sch.flush()

        attnT = carve(B0, [8, S], BF16)

        if stop_after >= 2:
            a = Bump(A0, B0)
            c = Bump(C0 + 32 * KB, CTOP)
            QT = [a([S], BF16) for _ in range(2)]
            KT = [a([S], BF16) for _ in range(2)]
            VH = [a([32, 128], BF16) for _ in range(2)]
            qtab = c([S], F32)
            wqb = c([2, 1024], BF16)
            wukb = c([512], BF16)
            wuvb = c([512], BF16)
            trib = c([128], BF16)
            rec = [c([512], F32) for _ in range(2)]
            PT = [c([512], BF16) for _ in range(6)]
            st_banks = [0, 1, 2, 6]
            dma("sp", qtab, qtab_d, (), ["qtab"])
            dma("pool", wqb, wq_d.rearrange("(k p) n -> p k n", p=128), (), ["wqb"])
            dma("pool", wukb, wuk_d, (), ["wukb"])
            dma("pool", wuvb, wuv_d, (), ["wuvb"])
            dma("pool", trib, tri_d, (), ["trib"])
            for hb in range(2):
                memset("pool", VH[hb][:, :, 64:128], 1.0, [("VH1", hb)])
            sc_mla = 1.0 / math.sqrt(96.0)
            prep_rr = [0]

            def prep_units(h):
                hb = h % 2
                units = []

                def q_unit(t):
                    tok = slice(t * 512, (t + 1) * 512)
                    b = 5
                    for k in range(2):
                        mm(banks[b][:, :], wqb[:, k, h * 128:(h + 1) * 128], cq[:, k, tok], k == 0, k == 1,
                           ["wqb", "cq"], [("ps", b)])
                    tt("dve", QT[hb][:, tok], banks[b][:, :], qtab[:, tok], ALU.mult,
                       [("ps", b), "qtab"], [("QT", hb)])

                def k_unit(t):
                    tok = slice(t * 512, (t + 1) * 512)
                    b = 5
                    mm(banks[b][0:64, :], wukb[:, h * 64:(h + 1) * 64], ckv[:, tok], True, True,
                       ["wukb", "ckv"], [("ps", b)])
                    cp("dve", KT[hb][0:64, tok], banks[b][0:64, :], [("ps", b)], [("KT", hb)])

                def kr_unit():
                    cp("pool", KT[hb][64:128, :], krd[64:128, :], ["krd"], [("KT", hb)])

                def v_unit(grp):
                    for i in range(8):
                        kt = grp * 8 + i
                        mm(banks[7][:, i * 64:(i + 1) * 64], ckv[:, kt * 128:(kt + 1) * 128],
                           wuvb[:, h * 64:(h + 1) * 64], True, True, ["wuvb", "ckv"], [("ps", 7)])
                    cp("dve", VH[hb][:, grp * 8:(grp + 1) * 8, 0:64],
                       banks[7][:, :].rearrange("p (a b) -> p a b", a=8, b=64), [("ps", 7)], [("VH", hb)])

                units.append(kr_unit)
                for t in range(8):
                    units.append(lambda t=t: q_unit(t))
                    units.append(lambda t=t: k_unit(t))
                for grp in range(4):
                    units.append(lambda grp=grp: v_unit(grp))
                return units

            st_rr = [0]
            pt_rr = [0]

            def attn(h, units):
                hb = h % 2
                tiles = []
                for qc in range(8):
                    for kt in range(4 * qc + 4):
                        tiles.append((qc, kt))
                pend = []

                def do_pv(item):
                    qc, kt, n0, N, pb = item
                    ob = 3 + (qc % 2)
                    mm(banks[ob][:, n0:512], VH[hb][:, kt, :], PT[pb][:, 0:N], kt == 0, kt == 4 * qc + 3,
                       [("VH", hb), ("VH1", hb), ("PT", pb)], [("ps", ob)])
                    if kt == 4 * qc + 3:
                        rb = qc % 2
                        recip(rec[rb][0:64, :], banks[ob][64:128, :], [("ps", ob)], [("rec", rb)])
                        po = (h % 2) * 64
                        tt("dve", attnT[po:po + 64, h // 2, qc * 512:(qc + 1) * 512], banks[ob][0:64, :],
                           rec[rb][0:64, :], ALU.mult, [("ps", ob), ("rec", rb)], [("attnT", h, qc)])

                for (qc, kt) in tiles:
                    n0 = max(0, kt * 128 - qc * 512)
                    N = 512 - n0
                    sb = st_banks[st_rr[0]]
                    st_rr[0] = (st_rr[0] + 1) % 4
                    pb = pt_rr[0]
                    pt_rr[0] = (pb + 1) % 6
                    diag = kt >= 4 * qc
                    mm(banks[sb][:, 0:N], KT[hb][:, kt * 128:(kt + 1) * 128],
                       QT[hb][:, qc * 512 + n0:(qc + 1) * 512], True, not diag,
                       [("KT", hb), ("QT", hb)], [("ps", sb)])
                    if diag:
                        mm(banks[sb][:, 0:128], identb, trib, False, True, ["ident", "trib"], [("ps", sb)])
                    act(PT[pb][:, 0:N], banks[sb][:, 0:N], AF.Exp, [("ps", sb)], [("PT", pb)], scale=sc_mla)
                    pend.append((qc, kt, n0, N, pb))
                    if len(pend) > 3:
                        do_pv(pend.pop(0))
                    n_t[0] += 1
                    if units and n_t[0] % 5 == 0:
                        units.pop(0)()
                while pend:
                    do_pv(pend.pop(0))
                while units:
                    units.pop(0)()

            n_t = [0]
            for u in prep_units(0):
                u()
            for h in range(8):
                attn(h, prep_units(h + 1) if h + 1 < 8 else [])
            dump("QT0", QT[0], [128, S], BF16, [("QT", 0)])
            dump("KT0", KT[0], [128, S], BF16, [("KT", 0)])
            dump("VH0", VH[0], [128, 32, 128], BF16, [("VH", 0)])
            dump("attn_mla", attnT[:, 0:4, :], [128, 4, S], BF16,
                 [("attnT", h, qc) for h in range(8) for qc in range(8)])
            sch.flush()

        if stop_after >= 3:
            a = Bump(A0, B0)
            c = Bump(C0, CTOP)
            qh = [a([S], BF16) for _ in range(2)]
            kh = [a([S], BF16) for _ in range(2)]
            VD = [a([32, 128], BF16) for _ in range(3)]
            rect = a([1024], F32)
            acc = [c([S], F32) for _ in range(2)]
            btab = [c([3, 256], BF16) for _ in range(2)]
            PTd = [c([512], BF16) for _ in range(6)]
            qkp = {1: (c([S], BF16), c([S], BF16)), 2: (c([S], BF16), c([S], BF16))}
            mtv = mtab_d.rearrange("p (a b) -> p a b", a=24, b=256)
            for di_ in (1, 2):
                for w_ in range(2):
                    memset("pool", qkp[di_][w_][64:128, :], 0.0, [("qkpz", di_, w_)])

            def load_btab(h):
                dma("pool", btab[h % 2], mtv[:, 3 * h:3 * h + 3, :], (), [("btab", h % 2)])

            def load_perm(h, di):
                src = qk4_s if di == 1 else qk16_s
                for w_ in range(2):
                    dma("sp", qkp[di][w_][0:64, :], src[w_][h * 64:(h + 1) * 64, :], (), [("qkp", di, w_)])
            for di in range(3):
                memset("pool", VD[di][:, :, 64:128], 1.0, [("VD1", di)])
            for b_ in range(2):
                memset("pool", qh[b_][64:128, :], 0.0, [("qz", b_)])
                memset("pool", kh[b_][64:128, :], 0.0, [("kz", b_)])

            def load_qk(h):
                hb = h % 2
                dma("sp", qh[hb][0:64, :], qd_s[h * 64:(h + 1) * 64, :], (), [("qh", hb)])
                dma("sp", kh[hb][0:64, :], kd_s[h * 64:(h + 1) * 64, :], (), [("kh", hb)])

            def load_v(h, di):
                dil = DILS[di]
                nb = 32 // dil
                src = vd_s.rearrange("(kb p r) c -> p r kb c", kb=nb, p=128, r=dil)
                dst = VD[di].rearrange("p (r kb) e -> p r kb e", r=dil, kb=nb)
                cols = slice(h * 64, (h + 1) * 64)
                if dil == 1:
                    for g in range(4):
                        dma("sp", dst[:, 0, g * 8:(g + 1) * 8, 0:64], src[:, 0, g * 8:(g + 1) * 8, cols], (), [("VD", di)])
                elif dil == 4:
                    for r in range(4):
                        dma("sp", dst[:, r, :, 0:64], src[:, r, :, cols], (), [("VD", di)])
                else:
                    for kb in range(2):
                        for g in range(2):
                            dma("sp", dst[:, g * 8:(g + 1) * 8, kb, 0:64], src[:, g * 8:(g + 1) * 8, kb, cols],
                                (), [("VD", di)])

            st_rr3 = [0]
            pt_rr3 = [0]
            ob_rr = [0]

            def dil_tiles(h, di):
                hb = h % 2
                dil = DILS[di]
                nb = 32 // dil
                gbank = {}
                gdone = {}

                def slot(r, n):
                    if dil == 16:
                        return r // 2, (r % 2) * 256 + n * 128
                    return r * (nb // 4) + n // 4, (n % 4) * 128

                def obank(gid):
                    if gid not in gbank:
                        gbank[gid] = 4 + ob_rr[0]
                        ob_rr[0] = (ob_rr[0] + 1) % 4
                        gdone[gid] = 0
                    return gbank[gid]

                def finish(gid):
                    ob = gbank[gid]
                    if dil == 1:
                        av = acc[hb][:, gid * 512:(gid + 1) * 512]
                        ov = banks[ob]
                    elif dil == 4:
                        r, g = gid // 2, gid % 2
                        av = acc[hb].rearrange("p (i r) -> p r i", r=4)[:, r, g * 512:(g + 1) * 512]
                        ov = banks[ob]
                    else:
                        av = acc[hb].rearrange("p (i r) -> p r i", r=16)[:, 2 * gid:2 * gid + 2, :]
                        ov = banks[ob].rearrange("p (a b) -> p a b", a=2, b=256)
                    tt("dve", av, ov, av, ALU.add, [("ps", ob), ("acc", hb)], [("acc", hb)], strict=True)

                def do_pv(item):
                    r, kb, pb, c0 = item
                    blk = r * nb + kb
                    rd = [("VD", di), ("VD1", di), ("PTd", pb)]
                    gid, col = slot(r, kb)
                    ob = obank(gid)
                    mm(banks[ob][:, col:col + 128], VD[di][:, blk, :], PTd[pb][:, c0:c0 + 128], kb == 0, True,
                       rd, [("ps", ob)])
                    if kb + 1 < nb:
                        gid2, col2 = slot(r, kb + 1)
                        ob2 = obank(gid2)
                        mm(banks[ob2][:, col2:col2 + 128], VD[di][:, blk, :], PTd[pb][:, c0 + 128:c0 + 256], True, False,
                           rd, [("ps", ob2)])
                    gdone[gid] += 1
                    if gdone[gid] == 4:
                        finish(gid)

                tiles = [(r, kb) for r in range(dil) for kb in range(nb)]
                pend = []
                for p0 in range(0, len(tiles), 2):
                    sb = st_rr3[0]
                    st_rr3[0] = (sb + 1) % 4
                    pb = pt_rr3[0]
                    pt_rr3[0] = (pb + 1) % 6
                    wend = 0
                    for t_i in range(2):
                        r, kb = tiles[p0 + t_i]
                        base = r + dil * 128 * kb
                        nq = 256 if kb + 1 < nb else 128
                        c0 = t_i * 256
                        wend = c0 + nq
                        if di == 0:
                            k_ap = kh[hb][:, base:base + 128]
                            q_ap = qh[hb][:, base:base + nq]
                            rd_ = [("kh", hb), ("qh", hb), ("qz", hb), ("kz", hb)]
                        else:
                            p0_ = r * (S // dil) + 128 * kb
                            k_ap = qkp[di][1][:, p0_:p0_ + 128]
                            q_ap = qkp[di][0][:, p0_:p0_ + nq]
                            rd_ = [("qkp", di, 0), ("qkp", di, 1), ("qkpz", di, 0), ("qkpz", di, 1)]
                        mm(banks[sb][:, c0:c0 + nq], k_ap, q_ap, True, False, rd_, [("ps", sb)])
                        mm(banks[sb][:, c0:c0 + nq], identb, btab[hb][:, di, 0:nq], False, True,
                           ["ident", ("btab", hb)], [("ps", sb)])
                        pend.append((r, kb, pb, c0))
                    act(PTd[pb][:, 0:wend], banks[sb][:, 0:wend], AF.Exp, [("ps", sb)], [("PTd", pb)], scale=0.125)
                    while len(pend) > 4:
                        do_pv(pend.pop(0))
                    n_pair[0] += 1
                    if norm_q and n_pair[0] % 2 == 0:
                        norm_q.pop(0)()
                while pend:
                    do_pv(pend.pop(0))

            norm_q = []
            n_pair = [0]

            def norm_ops(h, q4):
                hb = h % 2
                po = (h % 2) * 64
                cs = slice(q4 * 1024, (q4 + 1) * 1024)
                return [
                    lambda: cp("dve", rect[0:64, :], acc[hb][64:128, cs], [("acc", hb)], ["rect"]),
                    lambda: act(rect[0:64, :], rect[0:64, :], AF.Ln, ["rect"], ["rect"]),
                    lambda: act(rect[0:64, :], rect[0:64, :], AF.Exp, ["rect"], ["rect"], scale=-1.0),
                    lambda: tt("dve", attnT[po:po + 64, 4 + h // 2, cs], acc[hb][0:64, cs], rect[0:64, :], ALU.mult,
                               [("acc", hb), "rect"], [("attnD", h, q4)]),
                ]

            load_qk(0)
            load_btab(0)
            for di in range(3):
                load_v(0, di)
            load_perm(0, 1)
            load_perm(0, 2)
            for h in range(8):
                hb = h % 2
                memset("pool", acc[hb], 0.0, [("acc", hb)])
                if h + 1 < 8:
                    load_qk(h + 1)
                    load_btab(h + 1)
                for di in range(3):
                    dil_tiles(h, di)
                    if h + 1 < 8:
                        load_v(h + 1, di)
                        if di > 0:
                            load_perm(h + 1, di)
                for q4 in range(4):
                    norm_q.extend(norm_ops(h, q4))
            while norm_q:
                norm_q.pop(0)()
            dump("attn_all", attnT, [128, 8, S], BF16, [("attnD", h, q) for h in range(8) for q in range(4)])
            sch.flush()

        x1T = carve(A0, [8, S], BF16)

        def layer_norm(xr_ap, key, g_ap, b_ap, sm, smk):
            st6, mv, lnv, rs, nmr = sm
            for hf in range(2):
                sch.add("dve", lambda e, hf=hf: e.bn_stats(out=st6[:, hf, :], in_=xr_ap[:, hf * 512:(hf + 1) * 512]),
                        [key], [(smk, "st", hf)], strict=True)
            sch.add("dve", lambda e: e.bn_aggr(out=mv, in_=st6.rearrange("p a b -> p (a b)")),
                    [(smk, "st", 0), (smk, "st", 1)], [(smk, "mv")], strict=True)
            act(lnv, mv[:, 1:2], AF.Ln, [(smk, "mv"), "epsl"], [(smk, "lnv")], scale=1.0, bias=epsl, strict=True)
            act(rs, lnv, AF.Exp, [(smk, "lnv")], [(smk, "rs")], scale=-0.5, strict=True)
            ts("dve", nmr, mv[:, 0:1], rs, -1.0, ALU.mult, ALU.mult, [(smk, "mv"), (smk, "rs")], [(smk, "nmr")],
               strict=True)
            act(xr_ap, xr_ap, AF.Identity, [key, (smk, "rs"), (smk, "nmr")], [key], scale=rs, bias=nmr)
            tt("dve", xr_ap, xr_ap, g_ap, ALU.mult, [key, "lnp"], [key])
            tt("pool", xr_ap, xr_ap, b_ap, ALU.add, [key, "lnp"], [key])

        if stop_after >= 4:
            c = Bump(C0, CTOP)
            wob = c([8, 1024], BF16)
            lnp1 = c([2, 1024], F32)
            NXR = 8
            xr3 = [c([1024], F32) for _ in range(NXR)]
            yb = [c([1024], BF16) for _ in range(2)]
            NSM = 4
            sms = [(c([2, 6], F32), c([2], F32), c([1], F32), c([1], F32), c([1], F32)) for _ in range(NSM)]
            wov = wo_d.rearrange("(k p) n -> p k n", p=128)
            for k in range(0, 8, 4):
                dma("pool", wob[:, k:k + 4, :], wov[:, k:k + 4, :], (), [("wob", k // 4)])
            dma("sp", lnp1, lnp_d[0:2].rearrange("a p n -> p a n"), (), ["lnp"])

            def ln_stats(xr, key, sm, smk):
                st6, mv, lnv, rs, nmr = sm
                for hf in range(2):
                    sch.add("dve", lambda e, hf=hf: e.bn_stats(out=st6[:, hf, :], in_=xr[:, hf * 512:(hf + 1) * 512]),
                            [key], [(smk, "st", hf)], strict=True)
                sch.add("dve", lambda e: e.bn_aggr(out=mv, in_=st6.rearrange("p a b -> p (a b)")),
                        [(smk, "st", 0), (smk, "st", 1)], [(smk, "mv")], strict=True)

            def ln_rstd(sm, smk):
                st6, mv, lnv, rs, nmr = sm
                act(lnv, mv[:, 1:2], AF.Ln, [(smk, "mv"), "epsl"], [(smk, "lnv")], scale=1.0, bias=epsl, strict=True)
                act(rs, lnv, AF.Exp, [(smk, "lnv")], [(smk, "rs")], scale=-0.5, strict=True)
                ts("dve", nmr, mv[:, 0:1], rs, -1.0, ALU.mult, ALU.mult, [(smk, "mv"), (smk, "rs")], [(smk, "nmr")],
                   strict=True)

            def ln_norm(xr, key, sm, smk):
                st6, mv, lnv, rs, nmr = sm
                act(xr, xr, AF.Identity, [key, (smk, "rs"), (smk, "nmr")], [key], scale=rs, bias=nmr)

            def ln_affine(xr, key, g_ap, b_ap):
                tt("dve", xr, xr, g_ap, ALU.mult, [key, "lnp"], [key])
                tt("pool", xr, xr, b_ap, ALU.add, [key, "lnp"], [key])

            def s0(i):
                tk = slice(i * 128, (i + 1) * 128)
                pb = (i % 3) * 2
                dma("sp", xr3[i % NXR], x_d[tk, :], (), [("xr", i % NXR)])
                for hf in range(2):
                    for cc in range(8):
                        mm(banks[pb + hf], attnT[:, cc, tk], wob[:, cc, hf * 512:(hf + 1) * 512],
                           cc == 0, cc == 7, ["attnT", ("wob", 0), ("wob", 1)], [("ps", pb + hf)])

            def s1(i):
                pb = (i % 3) * 2
                key = ("xr", i % NXR)
                xr = xr3[i % NXR]
                for hf in range(2):
                    hs = slice(hf * 512, (hf + 1) * 512)
                    stt("dve", xr[:, hs], xr[:, hs], DN_ALPHA, banks[pb + hf], ALU.mult, ALU.add,
                        [key, ("ps", pb + hf)], [key])
                ln_stats(xr, key, sms[i % NSM], ("sm", i % NSM))

            def s2(i):
                ln_rstd(sms[i % NSM], ("sm", i % NSM))

            def s3(i):
                ln_norm(xr3[i % NXR], ("xr", i % NXR), sms[i % NSM], ("sm", i % NSM))

            def s4(i):
                ln_affine(xr3[i % NXR], ("xr", i % NXR), lnp1[:, 0, :], lnp1[:, 1, :])

            def s5(i):
                tk = slice(i * 128, (i + 1) * 128)
                key = ("xr", i % NXR)
                dma("sp", x1_s[tk, :], xr3[i % NXR], [key], [("x1s", i)])
                act(yb[i % 2], xr3[i % NXR], AF.Copy, [key], [("yb", i % 2)])

            def s6(i):
                tk = slice(i * 128, (i + 1) * 128)
                tb = 6 + (i % 2)
                pT = banks[tb].bitcast(BF16).rearrange("p (a b) -> p a b", a=8, b=128)
                for cc in range(8):
                    tr(pT[:, cc, :], yb[i % 2][:, cc * 128:(cc + 1) * 128], identb, [("yb", i % 2), "ident"], [("ps", tb)])
                cp("act", x1T[:, :, tk], pT, [("ps", tb)], [("x1T", i)])

            stages = [(s6, 7), (s5, 6), (s4, 5), (s3, 4), (s2, 3), (s1, 2), (s0, 0)]
            for t in range(32 + 7):
                for fn_, lag in stages:
                    i = t - lag
                    if 0 <= i < 32:
                        fn_(i)
            dump("x1T", x1T, [128, 8, S], BF16, [("x1T", i) for i in range(32)])
            sch.flush()

        if stop_after >= 5:
            bb = Bump(B0, C0)
            c = Bump(C0, CTOP)
            hmid = bb([NJ, 1024], BF16)
            wupb = [bb([8, 256], BF16) for _ in range(3)]
            yy = [[bb([1024], F32), c([1024], F32)] for _ in range(2)]
            sms = [(c([2, 6], F32), c([2], F32), c([1], F32), c([1], F32), c([1], F32)) for _ in range(2)]
            wdb = c([NJ, 1024], BF16)
            lnp2 = c([2, 1024], F32)
            cw = c([44, 4], F32)
            halo = c([44, 2], F32)
            xr2 = [c([1024], F32) for _ in range(2)]
            dma("sp", lnp2, lnp_d[2:4].rearrange("a p n -> p a n"), (), ["lnp"])
            dma("sp", cw, cw_d.rearrange("p (a b) -> p a b", a=44, b=4), (), ["cw"])
            memset("pool", halo, 0.0, ["halo"])
            wdv = wd_d.rearrange("(j p) n -> p j n", p=128)
            wupv = wup_d.rearrange("j (k p) n -> j p k n", p=128)

            def load_wd(piece):
                j0 = 2 * piece
                dma("pool", wdb[:, j0:j0 + 2, :], wdv[:, j0:j0 + 2, :], (), [("wdb", piece)])

            seq = [(s, j) for s in range(4) for j in range(NJ)]

            def load_wup(it):
                dma("pool", wupb[it % 3], wupv[seq[it][1]], (), [("wupb", it % 3)])

            def up_mm(it):
                s, j = seq[it]
                q = it % 2
                for hf in range(2):
                    for t2 in range(2):
                        b = 4 * q + 2 * hf + t2
                        tok0 = s * 1024 + t2 * 512
                        for k in range(8):
                            mm(banks[b], wupb[it % 3][:, k, hf * 128:(hf + 1) * 128], x1T[:, k, tok0:tok0 + 512],
                               k == 0, k == 7, [("wupb", it % 3), "x1T"], [("ps", b)])

            def up_conv(it):
                s, j = seq[it]
                q = it % 2
                for hf in range(2):
                    ch = j + NJ * hf
                    b0 = 4 * q + 2 * hf
                    U = psum[:, b0 * 512:(b0 + 2) * 512]
                    ur = [("ps", b0), ("ps", b0 + 1)]
                    y = yy[hf][q]
                    yk = ("yy", hf, q)
                    hk = ("halo", ch)
                    act(y, U, AF.Identity, ur + ["cw"], [yk], scale=cw[:, ch, 2:3], bias=cw[:, ch, 3:4])
                    stt("dve", y[:, 1:1024], U[:, 0:1023], cw[:, ch, 1:2], y[:, 1:1024], ALU.mult, ALU.add,
                        ur + ["cw", yk], [yk])
                    stt("dve", y[:, 2:1024], U[:, 0:1022], cw[:, ch, 0:1], y[:, 2:1024], ALU.mult, ALU.add,
                        ur + ["cw", yk], [yk])
                    stt("dve", y[:, 0:2], halo[:, ch, 0:2], cw[:, ch, 0:1], y[:, 0:2], ALU.mult, ALU.add,
                        [hk, "halo", "cw", yk], [yk], strict=True)
                    stt("dve", y[:, 0:1], halo[:, ch, 1:2], cw[:, ch, 1:2], y[:, 0:1], ALU.mult, ALU.add,
                        [hk, "halo", "cw", yk], [yk], strict=True)
                    cp("dve", halo[:, ch, :], U[:, 1022:1024], ur, [hk], strict=True)

            def up_gate(it):
                s, j = seq[it]
                q = it % 2
                act(yy[1][q], yy[1][q], AF.Gelu_apprx_tanh, [("yy", 1, q)], [("yy", 1, q)])
                tt("pool", hmid[:, j, :], yy[1][q], yy[0][q], ALU.mult, [("yy", 1, q), ("yy", 0, q)], [("hmid", j)])

            def load_x1(n):
                tk0 = (n // 8) * 1024 + (n % 8) * 128
                dma("sp", xr2[n % 2], x1_s[tk0:tk0 + 128, :], [("x1s", n)], [("xr", n % 2)])

            n_dn = [0]

            def down(s, sub):
                tk0 = s * 1024 + sub * 128
                tk = slice(tk0, tk0 + 128)
                n = s * 8 + sub
                xb2 = n % 2
                key = ("xr", xb2)
                pb = (n_dn[0] % 4) * 2
                n_dn[0] += 1
                for hf in range(2):
                    for j in range(NJ):
                        mm(banks[pb + hf], hmid[:, j, sub * 128:(sub + 1) * 128], wdb[:, j, hf * 512:(hf + 1) * 512],
                           j == 0, j == NJ - 1, [("hmid", j), ("wdb", j // 2)], [("ps", pb + hf)])
                if n + 1 < 32:
                    load_x1(n + 1)
                for hf in range(2):
                    hs = slice(hf * 512, (hf + 1) * 512)
                    stt("dve", xr2[xb2][:, hs], xr2[xb2][:, hs], DN_ALPHA, banks[pb + hf], ALU.mult, ALU.add,
                        [key, ("ps", pb + hf)], [key])
                layer_norm(xr2[xb2], key, lnp2[:, 0, :], lnp2[:, 1, :], sms[n % 2], ("sm", n % 2))
                dma("sp", out_d[tk, :], xr2[xb2], [key], [("out", n)])

            load_wup(0)
            load_wup(1)
            load_x1(0)
            for it in range(len(seq)):
                s, j = seq[it]
                if it + 2 < len(seq):
                    load_wup(it + 2)
                if 1 <= it <= NJ // 2:
                    load_wd(it - 1)
                up_mm(it)
                up_conv(it)
                if j > 0:
                    up_gate(it - 1)
                if j == NJ - 1:
                    up_gate(it)
                    for sub in range(8):
                        down(s, sub)
            sch.flush()
        else:
            sch.flush()
    build.last_log = sch.log
    return nc, dump_out


def _tables():
    f32 = np.float32
    half = 16
    freqs = np.power(f32(10000.0), -(np.arange(half, dtype=f32) / f32(half))).astype(f32)
    ang = (np.arange(S, dtype=f32)[:, None] * freqs[None, :]).astype(f32)
    cos = np.cos(ang).astype(f32).T
    sin = np.sin(ang).astype(f32).T
    cc = np.concatenate([cos, cos], 0)
    ss = np.concatenate([-sin, sin], 0)
    qtab = np.concatenate([np.ones((64, S), f32), cc, ss], 0)
    z = np.zeros((64, S), f32)
    ktab = np.stack([np.concatenate([z, cc, cc], 0), np.concatenate([z, ss, ss], 0)], 0)
    kk = np.arange(128)[:, None]
    qq = np.arange(128)[None, :]
    tri = np.where(kk <= qq, 0.0, -30000.0).astype(f32)
    slopes = 2.0 ** (-8.0 * np.arange(1, 9, dtype=np.float64) / 8.0)
    ql = np.arange(256)[None, :]
    off = ql - kk
    valid = (off >= 0) & (off <= 128)
    mt = np.zeros((128, 24, 256), f32)
    for h in range(8):
        for di, dil in enumerate(DILS):
            bias = -8.0 * slopes[h] * dil * off.astype(np.float64)
            mt[:, h * 3 + di, :] = np.where(valid, bias, -240000.0).astype(f32)
    return qtab, ktab, tri, mt.reshape(128, 24 * 256)


def _prep_shared(inp):
    f32 = np.float32
    w_in = np.asarray(inp["w_in"], f32)
    kr = w_in[:, 384:416]
    krs = np.concatenate([kr[:, 16:32], kr[:, 0:16]], 1)
    w1 = np.concatenate([w_in[:, 0:384], kr, kr, kr, kr, krs, krs, krs, krs, w_in[:, 416:1952]], 1)
    assert w1.shape == (D, 2176)
    w_uq = np.asarray(inp["w_uq"], f32)
    rope = w_uq[:, :, 64:96]
    swap = np.concatenate([rope[:, :, 16:32], rope[:, :, 0:16]], 2)
    wq = np.concatenate([w_uq, swap], 2).reshape(256, 1024)
    gq = np.zeros((128, 4), f32)
    gq[:, 0:2] = np.asarray(inp["g_cq"], f32).reshape(2, 128).T
    gq[:, 2] = np.asarray(inp["g_ckv"], f32)
    qtab, ktab, tri, mtab = _tables()
    w_up = np.asarray(inp["w_up"], f32)
    wup = np.stack([np.concatenate([w_up[:, j * 128:(j + 1) * 128], w_up[:, DFF + j * 128:DFF + (j + 1) * 128]], 1)
                    for j in range(NJ)], 0)
    conv_w = np.asarray(inp["conv_w"], f32)
    conv_b = np.asarray(inp["conv_b"], f32)
    cw = np.concatenate([conv_w, conv_b[None, :]], 0)
    cw = cw.reshape(4, 44, 128).transpose(2, 1, 0).reshape(128, 44 * 4)
    lnp = np.stack([np.broadcast_to(np.asarray(inp[k], f32)[None, :], (128, D))
                    for k in ("ln1_g", "ln1_b", "ln2_g", "ln2_b")], 0)
    c = np.ascontiguousarray
    return {
        "w1": c(w1), "wq": c(wq), "wuk": c(np.asarray(inp["w_uk"], f32).reshape(128, 512)),
        "wuv": c(np.asarray(inp["w_uv"], f32).reshape(128, 512)), "gq": gq, "qtab": c(qtab), "ktab": c(ktab),
        "tri": c(tri), "mtab": c(mtab), "wo": c(np.asarray(inp["w_o"], f32)), "wup": c(wup), "cw": c(cw),
        "wd": c(np.asarray(inp["w_down"], f32)), "lnp": c(lnp), "ident": np.eye(128, dtype=f32),
    }


def kernel(**inputs):
    x = np.asarray(inputs["x"], np.float32)
    B = x.shape[0]
    shared = _prep_shared(inputs)
    nc, _ = build()
    in_maps = []
    for b in range(B):
        m = dict(shared)
        m["x"] = np.ascontiguousarray(x[b])
        m["xT"] = np.ascontiguousarray(x[b].T)
        in_maps.append(m)
    res = run_bass_kernel_spmd(nc, in_maps, core_ids=list(range(B)))
    return np.stack([np.asarray(r["out"], np.float32) for r in res.results], 0)
```
